# Optimizing a Trainium2 kernel written in Bass

```python
import jax
import jax.numpy as jnp
from jax import lax
import numpy as np

D_MODEL = 2048
BATCH = 4
SEQ = 2048
DEPTH = 4
DEC_BATCH = 32
DEC_SEQ = 8
PAST_LEN = 16384
PAGE_SIZE = 128

D_FF = 5632
FFN_RES_WEIGHT = 0.5
POOL_WIDTH = D_MODEL // 2
POOL_WINDOWS = (2, 4, 8, 16)
POOL_GROUPS = len(POOL_WINDOWS)
POOL_GROUP_DIM = POOL_WIDTH // POOL_GROUPS
POOL_KEEP = max(POOL_WINDOWS) - 1
HEAD_DIM = 64
N_HEADS = D_MODEL // 128
N_KV_HEADS = 4
GQA_GROUP = N_HEADS // N_KV_HEADS
ATTN_WIDTH = N_HEADS * HEAD_DIM
KV_WIDTH = N_KV_HEADS * HEAD_DIM
WINDOW = 128
Q_BLOCK = 128
ROT_DIM = HEAD_DIM // 4
ROPE_THETA = 500000.0
LRU_WIDTH = D_MODEL // 2
LRU_BLOCKS = 16
LRU_BLOCK_DIM = LRU_WIDTH // LRU_BLOCKS
CONV_WIDTH = 4
LRU_C = 8.0
N_BRANCHES = 3
IN_WIDTHS = (POOL_WIDTH, ATTN_WIDTH, KV_WIDTH, KV_WIDTH, LRU_WIDTH, LRU_WIDTH, N_BRANCHES * D_MODEL)
IN_COLS = sum(IN_WIDTHS)
RMS_EPS = 1e-6
NEG_INF = -1e30

kernel_name = "hybrid_pool_swa_rglru_macaron_step"


def _split_points():
    pts, acc = [], 0
    for w in IN_WIDTHS[:-1]:
        acc += w
        pts.append(acc)
    return pts


def rms_norm(x, g):
    xf = x.astype(jnp.float32)
    y = xf * lax.rsqrt(jnp.mean(xf * xf, axis=-1, keepdims=True) + RMS_EPS)
    return (y * g.astype(jnp.float32)).astype(x.dtype)


def swiglu_ffn(x, w_gu, w_down):
    g, u = jnp.split(x @ w_gu, 2, axis=-1)
    return (jax.nn.silu(g) * u) @ w_down


def rotary(x, pos):
    half = ROT_DIM // 2
    inv = ROPE_THETA ** (-jnp.arange(half, dtype=jnp.float32) / half)
    ang = pos.astype(jnp.float32)[:, None] * inv[None, :]
    cos = jnp.cos(ang)[None, :, None, :]
    sin = jnp.sin(ang)[None, :, None, :]
    xr = x[..., :ROT_DIM].astype(jnp.float32)
    x1, x2 = xr[..., :half], xr[..., half:]
    rot = jnp.concatenate([x1 * cos - x2 * sin, x2 * cos + x1 * sin], axis=-1).astype(x.dtype)
    return jnp.concatenate([rot, x[..., ROT_DIM:]], axis=-1)


def pool_mixer(u, past, pos, w_grp, scale):
    B, T, _ = u.shape
    full = jnp.concatenate([past, u], axis=1)
    cs = jnp.cumsum(full.astype(jnp.float32), axis=1)
    cs = jnp.concatenate([jnp.zeros((B, 1, POOL_WIDTH), jnp.float32), cs], axis=1)
    means = []
    for g, w in enumerate(POOL_WINDOWS):
        lo_c, hi_c = g * POOL_GROUP_DIM, (g + 1) * POOL_GROUP_DIM
        hi = cs[:, POOL_KEEP + 1:POOL_KEEP + 1 + T, lo_c:hi_c]
        lo = cs[:, POOL_KEEP + 1 - w:POOL_KEEP + 1 - w + T, lo_c:hi_c]
        cnt = jnp.minimum(pos + 1, w).astype(jnp.float32)[None, :, None]
        means.append((hi - lo) / cnt)
    mean = jnp.concatenate(means, axis=-1).astype(u.dtype)
    d = (mean - u).reshape(B, T, POOL_GROUPS, POOL_GROUP_DIM)
    out = jnp.einsum('btgc,gcd->btgd', d, w_grp).reshape(B, T, POOL_WIDTH)
    return out * scale, full[:, -POOL_KEEP:]


def sliding_window_attention(q, k, v, k_past, v_past, pos0, sinks):
    B, T = q.shape[0], q.shape[1]
    qb = min(Q_BLOCK, T)
    nb = -(-T // qb)
    tp = nb * qb
    pad = ((0, 0), (0, tp - T), (0, 0), (0, 0))
    k_all = jnp.concatenate([k_past, jnp.pad(k, pad)], axis=1)
    v_all = jnp.concatenate([v_past, jnp.pad(v, pad)], axis=1)
    span = WINDOW + qb
    idx = jnp.arange(nb)[:, None] * qb + jnp.arange(span)[None, :]
    kb = jnp.take(k_all, idx, axis=1)
    vb = jnp.take(v_all, idx, axis=1)
    qblk = jnp.pad(q, pad).reshape(B, nb, qb, N_KV_HEADS, GQA_GROUP, HEAD_DIM)
    s = jnp.einsum('bnqkgd,bnskd->bnkgqs', qblk, kb).astype(jnp.float32) * (HEAD_DIM ** -0.5)
    q_pos = pos0 + jnp.arange(tp).reshape(nb, qb)
    k_pos = pos0 - WINDOW + idx
    rel = q_pos[:, :, None] - k_pos[:, None, :]
    valid = (k_pos[:, None, :] >= 0) & (rel >= 0) & (rel <= WINDOW)
    s = jnp.where(valid[None, :, None, None, :, :], s, NEG_INF)
    sink = sinks.astype(jnp.float32).reshape(N_KV_HEADS, GQA_GROUP)[None, None, :, :, None, None]
    m = jnp.maximum(jnp.max(s, axis=-1, keepdims=True), sink)
    p = jnp.exp(s - m)
    p = (p / (jnp.sum(p, axis=-1, keepdims=True) + jnp.exp(sink - m))).astype(v.dtype)
    o = jnp.einsum('bnkgqs,bnskd->bnqkgd', p, vb).reshape(B, tp, ATTN_WIDTH)[:, :T]
    return o, k_all[:, T:T + WINDOW], v_all[:, T:T + WINDOW]


def rglru_mixer(xb, g_in, conv_past, h0, conv_w, conv_b, wa, ba, wx, bx, lam):
    B, T, _ = xb.shape
    full = jnp.concatenate([conv_past, xb], axis=1)
    xc = conv_b + full[:, 0:T] * conv_w[0]
    for j in range(1, CONV_WIDTH):
        xc = xc + full[:, j:j + T] * conv_w[j]
    xr = xc.reshape(B, T, LRU_BLOCKS, LRU_BLOCK_DIM)
    r = jax.nn.sigmoid((jnp.einsum('bthc,hcd->bthd', xr, wa).reshape(B, T, LRU_WIDTH) + ba).astype(jnp.float32))
    i = jax.nn.sigmoid((jnp.einsum('bthc,hcd->bthd', xr, wx).reshape(B, T, LRU_WIDTH) + bx).astype(jnp.float32))
    log_a = -LRU_C * r * jax.nn.softplus(-lam.astype(jnp.float32))
    a = jnp.exp(log_a)
    b = jnp.sqrt(-jnp.expm1(2.0 * log_a)) * (i * xc.astype(jnp.float32))
    b = b.at[:, 0].add(a[:, 0] * h0.astype(jnp.float32))

    def combine(left, right):
        a1, b1 = left
        a2, b2 = right
        return a1 * a2, a2 * b1 + b2

    _, h = lax.associative_scan(combine, (a, b), axis=1)
    y = h.astype(xb.dtype) * jax.nn.gelu(g_in)
    return y, full[:, -(CONV_WIDTH - 1):], h[:, -1].astype(h0.dtype)


def trunk_layer(x, pos0, pool_past, k_past, v_past, conv_past, h0, lw):
    B, T, _ = x.shape
    x = x + FFN_RES_WEIGHT * swiglu_ffn(rms_norm(x, lw['norm_ffa']), lw['ffa_w_gu'], lw['ffa_w_down'])
    h = rms_norm(x, lw['norm_mix'])
    z = h @ lw['w_in']
    u_pool, q, k, v, x_lru, g_lru, gates = jnp.split(z, _split_points(), axis=-1)
    pos = pos0 + jnp.arange(T)
    pool_out, new_pool = pool_mixer(u_pool, pool_past, pos, lw['pool_w'], lw['pool_scale'])
    q = rotary(rms_norm(q.reshape(B, T, N_HEADS, HEAD_DIM), lw['q_norm']), pos)
    k = rotary(rms_norm(k.reshape(B, T, N_KV_HEADS, HEAD_DIM), lw['k_norm']), pos)
    v = v.reshape(B, T, N_KV_HEADS, HEAD_DIM)
    attn_out, new_k, new_v = sliding_window_attention(q, k, v, k_past, v_past, pos0, lw['attn_sinks'])
    lru_out, new_conv, new_h = rglru_mixer(x_lru, g_lru, conv_past, h0, lw['conv_w'], lw['conv_b'],
                                           lw['lru_gate_a_w'], lw['lru_gate_a_b'],
                                           lw['lru_gate_x_w'], lw['lru_gate_x_b'], lw['lru_lambda'])
    g = jax.nn.sigmoid(gates.reshape(B, T, N_BRANCHES, D_MODEL))
    merged = (g[:, :, 0] * (pool_out @ lw['w_branch_pool'])
              + g[:, :, 1] * (attn_out @ lw['w_branch_attn'])
              + g[:, :, 2] * (lru_out @ lw['w_branch_lru']))
    x = x + merged @ lw['w_out']
    x = x + FFN_RES_WEIGHT * swiglu_ffn(rms_norm(x, lw['norm_ffb']), lw['ffb_w_gu'], lw['ffb_w_down'])
    return x, (new_pool, new_k, new_v, new_conv, new_h)


def setup_inputs(seed: int = 0) -> dict:
    key = jax.random.key(seed)
    ks = iter(jax.random.split(key, 48))
    f32 = jnp.float32

    def nrm(shape, scale):
        return scale * jax.random.normal(next(ks), shape, f32)

    def gain(shape):
        return 1.0 + nrm(shape, 0.02)

    sw_keep = min(WINDOW, PAST_LEN)
    a8 = jax.random.uniform(next(ks), (DEPTH, LRU_WIDTH), f32, 0.9, 0.999)
    a_base = a8 ** (1.0 / LRU_C)
    return {
        'x_prompt': nrm((BATCH, SEQ, D_MODEL), 1.0),
        'x_sample': nrm((DEC_BATCH, DEC_SEQ, D_MODEL), 1.0),
        'state_pool': nrm((DEPTH, DEC_BATCH, POOL_KEEP, POOL_WIDTH), 1.0),
        'cache_k_win': nrm((DEPTH, DEC_BATCH, sw_keep, N_KV_HEADS, HEAD_DIM), 1.0),
        'cache_v_win': nrm((DEPTH, DEC_BATCH, sw_keep, N_KV_HEADS, HEAD_DIM), 1.0),
        'state_conv': nrm((DEPTH, DEC_BATCH, CONV_WIDTH - 1, LRU_WIDTH), 1.0),
        'state_rglru': nrm((DEPTH, DEC_BATCH, LRU_WIDTH), 0.5),
        'norm_ffa': gain((DEPTH, D_MODEL)),
        'ffa_w_gu': nrm((DEPTH, D_MODEL, 2 * D_FF), D_MODEL ** -0.5),
        'ffa_w_down': nrm((DEPTH, D_FF, D_MODEL), D_FF ** -0.5),
        'norm_mix': gain((DEPTH, D_MODEL)),
        'w_in': nrm((DEPTH, D_MODEL, IN_COLS), D_MODEL ** -0.5),
        'pool_w': nrm((DEPTH, POOL_GROUPS, POOL_GROUP_DIM, POOL_GROUP_DIM), POOL_GROUP_DIM ** -0.5),
        'pool_scale': gain((DEPTH, POOL_WIDTH)),
        'q_norm': gain((DEPTH, HEAD_DIM)),
        'k_norm': gain((DEPTH, HEAD_DIM)),
        'attn_sinks': nrm((DEPTH, N_HEADS), 0.5),
        'conv_w': nrm((DEPTH, CONV_WIDTH, LRU_WIDTH), CONV_WIDTH ** -0.5),
        'conv_b': nrm((DEPTH, LRU_WIDTH), 0.01),
        'lru_gate_a_w': nrm((DEPTH, LRU_BLOCKS, LRU_BLOCK_DIM, LRU_BLOCK_DIM), LRU_BLOCK_DIM ** -0.5),
        'lru_gate_a_b': nrm((DEPTH, LRU_WIDTH), 0.01),
        'lru_gate_x_w': nrm((DEPTH, LRU_BLOCKS, LRU_BLOCK_DIM, LRU_BLOCK_DIM), LRU_BLOCK_DIM ** -0.5),
        'lru_gate_x_b': nrm((DEPTH, LRU_WIDTH), 0.01),
        'lru_lambda': jnp.log(a_base) - jnp.log1p(-a_base),
        'w_branch_pool': nrm((DEPTH, POOL_WIDTH, D_MODEL), POOL_WIDTH ** -0.5),
        'w_branch_attn': nrm((DEPTH, ATTN_WIDTH, D_MODEL), ATTN_WIDTH ** -0.5),
        'w_branch_lru': nrm((DEPTH, LRU_WIDTH, D_MODEL), LRU_WIDTH ** -0.5),
        'w_out': nrm((DEPTH, D_MODEL, D_MODEL), D_MODEL ** -0.5),
        'norm_ffb': gain((DEPTH, D_MODEL)),
        'ffb_w_gu': nrm((DEPTH, D_MODEL, 2 * D_FF), D_MODEL ** -0.5),
        'ffb_w_down': nrm((DEPTH, D_FF, D_MODEL), D_FF ** -0.5),
    }


def reference(x_prompt, x_sample, state_pool, cache_k_win, cache_v_win, state_conv, state_rglru,
              norm_ffa, ffa_w_gu, ffa_w_down, norm_mix, w_in, pool_w, pool_scale, q_norm, k_norm,
              attn_sinks, conv_w, conv_b, lru_gate_a_w, lru_gate_a_b, lru_gate_x_w, lru_gate_x_b,
              lru_lambda, w_branch_pool, w_branch_attn, w_branch_lru, w_out, norm_ffb, ffb_w_gu,
              ffb_w_down):
    bp = x_prompt.shape[0]
    dt = x_prompt.dtype
    z_pool = jnp.zeros((bp, POOL_KEEP, POOL_WIDTH), dt)
    z_kv = jnp.zeros((bp, WINDOW, N_KV_HEADS, HEAD_DIM), dt)
    z_conv = jnp.zeros((bp, CONV_WIDTH - 1, LRU_WIDTH), dt)
    z_h = jnp.zeros((bp, LRU_WIDTH), state_rglru.dtype)

    yp, ys = x_prompt, x_sample
    st_p = ([], [], [], [], [])
    st_s = ([], [], [], [], [])
    for l in range(DEPTH):
        lw = {
            'norm_ffa': norm_ffa[l], 'ffa_w_gu': ffa_w_gu[l], 'ffa_w_down': ffa_w_down[l],
            'norm_mix': norm_mix[l], 'w_in': w_in[l], 'pool_w': pool_w[l], 'pool_scale': pool_scale[l],
            'q_norm': q_norm[l], 'k_norm': k_norm[l], 'attn_sinks': attn_sinks[l],
            'conv_w': conv_w[l], 'conv_b': conv_b[l],
            'lru_gate_a_w': lru_gate_a_w[l], 'lru_gate_a_b': lru_gate_a_b[l],
            'lru_gate_x_w': lru_gate_x_w[l], 'lru_gate_x_b': lru_gate_x_b[l],
            'lru_lambda': lru_lambda[l],
            'w_branch_pool': w_branch_pool[l], 'w_branch_attn': w_branch_attn[l],
            'w_branch_lru': w_branch_lru[l], 'w_out': w_out[l],
            'norm_ffb': norm_ffb[l], 'ffb_w_gu': ffb_w_gu[l], 'ffb_w_down': ffb_w_down[l],
        }
        yp, sp = trunk_layer(yp, 0, z_pool, z_kv, z_kv, z_conv, z_h, lw)
        ys, ss = trunk_layer(ys, PAST_LEN, state_pool[l], cache_k_win[l], cache_v_win[l],
                             state_conv[l], state_rglru[l], lw)
        for j in range(5):
            st_p[j].append(sp[j])
            st_s[j].append(ss[j])

    new_pool_p = jnp.stack(st_p[0])
    new_pool_s = jnp.stack(st_s[0])
    new_k_p = jnp.stack(st_p[1])
    new_k_s = jnp.stack(st_s[1])
    new_v_p = jnp.stack(st_p[2])
    new_v_s = jnp.stack(st_s[2])
    new_conv_p = jnp.stack(st_p[3])
    new_conv_s = jnp.stack(st_s[3])
    new_h_p = jnp.stack(st_p[4])
    new_h_s = jnp.stack(st_s[4])
    return (yp, ys, new_pool_p, new_pool_s, new_k_p, new_k_s, new_v_p, new_v_s,
            new_conv_p, new_conv_s, new_h_p, new_h_s)
```

```python
import numpy as np
import concourse.bass as bass
import concourse.mybir as mybir
from concourse.bass_utils import run_bass_kernel_spmd

F32 = mybir.dt.float32
BF16 = mybir.dt.bfloat16
AF = mybir.ActivationFunctionType
ALU = mybir.AluOpType

NCORES = 8
DEPTH = 4
D = 2048
KC = 16
TP = 1024
NSQ = 4
SQ = 8
TS = NSQ * SQ
T = TP + TS
TG = 352
NTG = 3
DFF = 5632
NFC = DFF // 128
INC = 10752
C_POOL, C_Q, C_K, C_V, C_XL, C_GL, C_GATE = 0, 1024, 2048, 2304, 2560, 3584, 4608
PAST = 16384
EPS = 1e-6
VL = 138
V_NFFA, V_NMIX, V_NFFB, V_PSC, V_CW, V_CB, V_BA, V_BX, V_LAM, V_QN, V_KN, V_SNK = 0, 16, 32, 48, 56, 88, 96, 104, 112, 120, 121, 122
CS_FLAG, CS_INV, CS_ROT, CS_OBD, CS_MP0, CS_MP, CS_MC, CS_MSC, CS_MSN, NCST = 0, 2, 66, 194, 322, 450, 578, 706, 714, 748

X0 = 0
H0 = 16896
W0 = 25344
WSZ = 21312
P0 = W0 + WSZ
A_SQ = P0
A_RSTD = A_SQ + 1056
A_VEC = A_RSTD + 1056
A_CST = A_VEC + DEPTH * VL
A_CB = A_CST + NCST
CB_ONES, CB_OBD, CB_MP0, CB_MP, CB_MC, CB_MSC, CB_MSN = 0, 64, 128, 384, 640, 896, 912
A_ESNK = A_CB + 976
A_SP = A_ESNK + 16
A_END = A_SP + 16
NW = A_END


class Tok:
    __slots__ = ("sem", "val", "eng", "sid")

    def __init__(self, sem, val, eng, sid):
        self.sem, self.val, self.eng, self.sid = sem, val, eng, sid


class Sched:
    def __init__(self, nc):
        self.nc = nc
        self.E = {}
        self.lastw = {}
        self.readers = {}
        self.nsem = 0
        self.all_dsems = []

    def add_engine(self, name, eng, sem):
        self.E[name] = dict(eng=eng, sem=sem, cnt=0, waited={}, sid=self._sid())

    def _sid(self):
        self.nsem += 1
        return self.nsem

    def dsem(self, sem):
        d = dict(sem=sem, cnt=0, sid=self._sid())
        self.all_dsems.append(d)
        return d

    def op(self, ename, fn, reads=(), writes=(), dsem=None, inc=True, incval=16):
        E = self.E[ename]
        deps = []
        for k in reads:
            t = self.lastw.get(k)
            if t is not None:
                deps.append(t)
            if isinstance(k, tuple) and k[0] == "ps":
                r = self.readers.get(k)
                if r:
                    deps.extend(x for x in r.values() if x.eng != ename)
        for k in writes:
            t = self.lastw.get(k)
            if t is not None:
                deps.append(t)
            r = self.readers.get(k)
            if r:
                deps.extend(r.values())
        best = {}
        for t in deps:
            if t.eng == ename and ename == "pe":
                continue
            b = best.get(t.sid)
            if b is None or t.val > b.val:
                best[t.sid] = t
        for t in best.values():
            if E["waited"].get(t.sid, 0) >= t.val:
                continue
            E["eng"].wait_ge(t.sem, t.val)
            E["waited"][t.sid] = t.val
        inst = fn(E["eng"])
        tok = None
        if dsem is not None:
            dsem["cnt"] += incval
            inst.then_inc(dsem["sem"], incval)
            tok = Tok(dsem["sem"], dsem["cnt"], "dma", dsem["sid"])
        elif inc:
            E["cnt"] += 1
            inst.then_inc(E["sem"], 1)
            tok = Tok(E["sem"], E["cnt"], ename, E["sid"])
        if tok is not None:
            for k in reads:
                self.readers.setdefault(k, {})[tok.sid] = tok
            for k in writes:
                self.lastw[k] = tok
                self.readers[k] = {}
        return tok


def build(nlayers=DEPTH):
    nc = bass.Bass("TRN2", target_bir_lowering=False)
    LN = nlayers

    import os as _os0
    SMALL = set(x for x in _os0.environ.get("KSMALL", "").split(",") if x)

    def din(name, shape):
        if name in SMALL:
            shape = [1, 16]
        return nc.dram_tensor(name, list(shape), F32, kind="ExternalInput").ap()

    def dout(name, shape):
        return nc.dram_tensor(name, list(shape), F32, kind="ExternalOutput").ap()

    d_x = din("xT", [128, KC * T])
    d_wgu = [din("ffa_w_gu", [LN, D, 2 * DFF]), din("ffb_w_gu", [LN, D, 2 * DFF])]
    d_wd = [din("ffa_w_down", [LN, DFF, D]), din("ffb_w_down", [LN, DFF, D])]
    d_win = din("w_in", [LN, D, INC])
    d_wbr = [din("w_branch_pool", [LN, 1024, D]), din("w_branch_attn", [LN, 1024, D]), din("w_branch_lru", [LN, 1024, D])]
    d_wout = din("w_out", [LN, D, D])
    d_poolw = din("poolw", [128, LN * 4 * 2 * 256])
    d_lruw = din("lruw", [128, LN * 2 * 8 * 128])
    d_vecs = din("vecs", [128, LN * VL])
    d_cst = din("cst", [128, NCST])
    d_cos = din("cosT", [128, T])
    d_sin = din("sinT", [128, T])
    d_pool_s = din("pool_s", [128, LN * 8 * NSQ * 15])
    d_conv_s = din("conv_s", [128, LN * 8 * NSQ * 3])
    d_h_s = din("h_s", [128, LN * 8 * NSQ])
    d_kcT = din("kcT", [128, LN * NSQ * 4 * 128])
    d_vc = din("vc", [128, LN * NSQ * 4 * 128])
    d_kc_nat = din("kc_nat", [LN, NSQ, 128, 256])
    d_vc_nat = din("vc_nat", [LN, NSQ, 128, 256])
    d_pool_nat = din("pool_nat", [LN, NSQ, 15, 1024])

    o_y = dout("yT", [128, KC * T])
    o_pool_p = dout("o_pool_p", [128, LN * 8 * 15])
    o_pool_sn = dout("o_pool_sn", [128, LN * 8 * TS])
    o_pool_so = dout("o_pool_so", [LN, NSQ, 7, 1024])
    o_k_p = dout("o_k_p", [128, LN * 4 * 128])
    o_k_sn = dout("o_k_sn", [128, LN * 4 * TS])
    o_k_so = dout("o_k_so", [LN, NSQ, 120, 256])
    o_v_p = dout("o_v_p", [128, LN * 256])
    o_v_sn = dout("o_v_sn", [TS, LN * 256])
    o_v_so = dout("o_v_so", [LN, NSQ, 120, 256])
    o_conv_p = dout("o_conv_p", [128, LN * 8 * 3])
    o_xb_s = dout("o_xb_s", [128, LN * 8 * TS])
    o_h_p = dout("o_h_p", [128, LN * 8])
    o_h_s = dout("o_h_s", [128, LN * 8 * TS])

    xspill = nc.dram_tensor("xspill", [128, KC * T], F32).ap()
    groups = [[0, 1], [2, 3], [4, 5], [6, 7]]

    import contextlib
    with contextlib.ExitStack() as es:
        arena = es.enter_context(nc.sbuf_tensor("arena", [128, NW], F32))
        banks = [es.enter_context(nc.psum_tensor(f"bank{i}", [128, 512], F32)) for i in range(8)]
        sems = {}

        def newsem(name):
            s = es.enter_context(nc.semaphore(name))
            sems[name] = s
            return s

        S = Sched(nc)
        all_dsems = S.all_dsems
        block = es.enter_context(nc.Block())
        S.add_engine("pe", nc.tensor, newsem("s_pe"))
        S.add_engine("act", nc.scalar, newsem("s_act"))
        S.add_engine("dve", nc.vector, newsem("s_dve"))
        S.add_engine("pool", nc.gpsimd, newsem("s_pool"))
        S.add_engine("sp", nc.sync, newsem("s_sp"))

        def fv(off, n):
            return arena[:, off:off + n]

        def bv(off, nwords):
            return arena[:, off:off + nwords].bitcast(BF16)

        Xv = fv(X0, KC * T).rearrange("p (k t) -> p k t", k=KC)
        Hv = bv(H0, KC * T // 2).rearrange("p (k t) -> p k t", k=KC)
        SQv = [bv(A_SQ + i * 528, 528) for i in range(2)]
        RSTD = fv(A_RSTD, T)
        VEC = fv(A_VEC, DEPTH * VL)
        CST = fv(A_CST, NCST)
        ONES = bv(A_CB + CB_ONES, 64)
        OBD = bv(A_CB + CB_OBD, 64)
        MP0 = bv(A_CB + CB_MP0, 256)
        MP = bv(A_CB + CB_MP, 256)
        MC = bv(A_CB + CB_MC, 256)
        MSC = bv(A_CB + CB_MSC, 16)
        MSN = bv(A_CB + CB_MSN, 64)
        ESNK = fv(A_ESNK, 16)
        SPV = fv(A_SP, 16)
        FLAG = CST[:, CS_FLAG:CS_FLAG + 1]
        ROTM = CST[:, CS_ROT:CS_ROT + 128]

        def vec(l, off, n=1):
            return VEC[:, l * VL + off:l * VL + off + n]

        sp_sems = [S.dsem(newsem(f"s_spd{i}")) for i in range(8)]
        sp_rr = [0]
        sp_last = [None] * 8

        def sp_dma(out, in_, reads=(), writes=()):
            i = sp_rr[0] % 8
            sp_rr[0] += 1
            ds = sp_sems[i]
            E = S.E["sp"]
            if ds["cnt"] > 0 and E["waited"].get(ds["sid"], 0) < ds["cnt"]:
                nc.sync.wait_ge(ds["sem"], ds["cnt"])
                E["waited"][ds["sid"]] = ds["cnt"]
            return S.op("sp", lambda e: e.dma_start(out=out, in_=in_), reads=reads, writes=writes, dsem=ds)

        class Stream:
            def __init__(self, name, offs):
                self.name = name
                self.offs = offs
                self.ds = [S.dsem(newsem(f"s_{name}{i}")) for i in range(len(offs))]
                self.n = 0

            def load(self, parts):
                s = self.n % len(self.offs)
                self.n += 1
                key = (self.name, s)
                for j, (dst, src) in enumerate(parts):
                    S.op("pool", lambda e, dst=dst, src=src: e.dma_start(out=dst(self.offs[s]), in_=src),
                         writes=(key,), dsem=self.ds[s])
                return s

        ring = [0]

        def nextbank():
            i = ring[0] % 8
            ring[0] += 1
            return i

        def mm(bi, pairs, reads, out_ap=None, extra_w=()):
            n = len(pairs)
            o = out_ap if out_ap is not None else banks[bi][:, 0:TG]
            tok = None
            for j, (lt, rh) in enumerate(pairs):
                first, last = j == 0, j == n - 1
                tok = S.op("pe", lambda e, lt=lt, rh=rh, first=first, last=last: e.matmul(o, lt, rh, start=first, stop=last),
                           reads=reads if first else (), writes=((("ps", bi),) + tuple(extra_w)) if first else (), inc=last)
            S.lastw[("ps", bi)] = tok
            S.readers[("ps", bi)] = {}
            for k in reads:
                S.readers.setdefault(k, {})[tok.sid] = tok
            return tok

        def act(fn, reads, writes):
            return S.op("act", fn, reads=reads, writes=writes)

        def dve(fn, reads, writes):
            return S.op("dve", fn, reads=reads, writes=writes)

        def tgs(tg):
            return slice(tg * TG, (tg + 1) * TG)

        Xk = lambda k, tg: ("X", k, tg)


        def barrier_all(engs=("pe", "act", "dve", "pool", "sp")):
            toks = []
            for en in ("pe", "act", "dve"):
                E = S.E[en]
                if E["cnt"] > 0:
                    toks.append(Tok(E["sem"], E["cnt"], en, E["sid"]))
            for ds in all_dsems:
                if ds["cnt"] > 0:
                    toks.append(Tok(ds["sem"], ds["cnt"], "dma", ds["sid"]))
            for en in engs:
                E = S.E[en]
                for t in toks:
                    if E["waited"].get(t.sid, 0) >= t.val:
                        continue
                    E["eng"].wait_ge(t.sem, t.val)
                    E["waited"][t.sid] = t.val

        sp_dma(VEC[:, 0:LN * VL], d_vecs[:, :], writes=("vec",))
        sp_dma(CST, d_cst[:, :], writes=("cst",))
        sp_dma(fv(X0, KC * T), d_x[:, :], writes=[Xk(k, tg) for k in range(KC) for tg in range(NTG)])
        dve(lambda e: e.memset(ONES, 1.0), (), ("cb",))
        dve(lambda e: e.tensor_copy(out=OBD, in_=CST[:, CS_OBD:CS_OBD + 128]), ("cst",), ("cb",))
        for (dst, so) in ((MP0, CS_MP0), (MP, CS_MP), (MC, CS_MC)):
            for r in range(4):
                dve(lambda e, dst=dst, so=so, r=r: e.tensor_copy(out=dst[:, r * 128:(r + 1) * 128], in_=CST[:, so:so + 128]), ("cst",), ("cb",))
        for r in range(4):
            dve(lambda e, r=r: e.tensor_copy(out=MSC[:, r * 8:(r + 1) * 8], in_=CST[:, CS_MSC:CS_MSC + 8]), ("cst",), ("cb",))
            dve(lambda e, r=r: e.tensor_copy(out=MSN[0:32, r * 32:(r + 1) * 32], in_=CST[0:32, CS_MSN:CS_MSN + 32]), ("cst",), ("cb",))

        def rmsnorm(l, goff):
            bs = [nextbank() for _ in range(NTG)]
            for k in range(KC):
                sq = SQv[k % 2]
                act(lambda e, k=k, sq=sq: e.activation(out=sq, in_=Xv[:, k, :], func=AF.Square),
                    [Xk(k, tg) for tg in range(NTG)], (("sq", k % 2),))
                for tg in range(NTG):
                    S.op("pe", lambda e, tg=tg, sq=sq, k=k: e.matmul(banks[bs[tg]][:, 0:TG], ONES, sq[:, tgs(tg)], start=(k == 0), stop=(k == KC - 1)),
                         reads=(("sq", k % 2), "cb"), writes=(("ps", bs[tg]),) if k == 0 else (), inc=True)
                    if k == KC - 1:
                        S.lastw[("ps", bs[tg])] = Tok(S.E["pe"]["sem"], S.E["pe"]["cnt"], "pe", S.E["pe"]["sid"])
            for tg in range(NTG):
                act(lambda e, tg=tg: e.activation(out=RSTD[:, tgs(tg)], in_=banks[bs[tg]][:, 0:TG], func=AF.Sqrt, scale=1.0 / D, bias=EPSB),
                    (("ps", bs[tg]),), (("rstd", tg),))
                dve(lambda e, tg=tg: e.reciprocal(out=RSTD[:, tgs(tg)], in_=RSTD[:, tgs(tg)]), (("rstd", tg),), (("rstd", tg),))
            for k in range(KC):
                dve(lambda e, k=k: e.scalar_tensor_tensor(out=Hv[:, k, :], in0=Xv[:, k, :], scalar=vec(l, goff + k), in1=RSTD, op0=ALU.mult, op1=ALU.mult),
                    [Xk(k, tg) for tg in range(NTG)] + [("rstd", tg) for tg in range(NTG)] + ["vec"], (("H",),))

        EPSB = fv(A_SP + 15, 1)
        dve(lambda e: e.memset(EPSB, EPS), (), ("epsb",))

        WGU_OFF = [W0, W0 + 4096]
        WD_OFF = [W0 + 8192, W0 + 12288]
        ACT_OFF = [W0 + 16384, W0 + 16384 + 2112]
        SIL_OFF = [W0 + 20608, W0 + 20608 + 352]
        st_wgu = Stream("wgu", WGU_OFF)
        st_wd = Stream("wd", WD_OFF)

        def ffn(l, which):
            wgu_d, wd_d = d_wgu[which], d_wd[which]
            barrier_all(engs=("pool",))

            def load_gu(u):
                c0 = u * 256
                gsrc = wgu_d[l, :, c0:c0 + 256].rearrange("(k p) c -> p k c", p=128)
                usrc = wgu_d[l, :, DFF + c0:DFF + c0 + 256].rearrange("(k p) c -> p k c", p=128)
                return st_wgu.load([
                    (lambda off: bv(off, 4096).rearrange("p (k c) -> p k c", k=KC)[:, :, 0:256], gsrc),
                    (lambda off: bv(off, 4096).rearrange("p (k c) -> p k c", k=KC)[:, :, 256:512], usrc)])

            def load_d(g):
                src = wd_d[l, g * 512:(g + 1) * 512, :].rearrange("(c p) n -> p c n", p=128)
                return st_wd.load([(lambda off: bv(off, 4096).rearrange("p (c n) -> p c n", c=4), src)])

            NU = NFC // 2
            NG = NFC // 4
            slots_gu = {0: load_gu(0)}
            slots_d = {0: load_d(0)}
            sil_i = [0]
            for g in range(NG):
                actb = bv(ACT_OFF[g % 2], 2112).rearrange("p (c t) -> p c t", c=4)
                for half in range(2):
                    u = g * 2 + half
                    if u + 1 < NU:
                        slots_gu[u + 1] = load_gu(u + 1)
                    s = slots_gu[u]
                    wv = bv(WGU_OFF[s], 4096).rearrange("p (k c) -> p k c", k=KC)
                    for ci in range(2):
                        for tg in range(NTG):
                            bg, bu = nextbank(), nextbank()
                            mm(bg, [(wv[:, k, ci * 128:(ci + 1) * 128], Hv[:, k, tgs(tg)]) for k in range(KC)], reads=(("wgu", s), ("H",)))
                            mm(bu, [(wv[:, k, 256 + ci * 128:256 + (ci + 1) * 128], Hv[:, k, tgs(tg)]) for k in range(KC)], reads=(("wgu", s), ("H",)))
                            si = sil_i[0] % 2
                            sil_i[0] += 1
                            sil = fv(SIL_OFF[si], TG)
                            act(lambda e, sil=sil, bg=bg: e.activation(out=sil, in_=banks[bg][:, 0:TG], func=AF.Silu), (("ps", bg),), (("sil", si),))
                            cc = half * 2 + ci
                            dve(lambda e, sil=sil, bu=bu, cc=cc, tg=tg, actb=actb: e.tensor_tensor(out=actb[:, cc, tgs(tg)], in0=sil, in1=banks[bu][:, 0:TG], op=ALU.mult),
                                (("sil", si), ("ps", bu)), (("actb", g % 2, cc, tg),))
                if g + 1 < NG:
                    slots_d[g + 1] = load_d(g + 1)
                sd = slots_d[g]
                wdv = bv(WD_OFF[sd], 4096).rearrange("p (c n) -> p c n", c=4)
                for d in range(KC):
                    for tg in range(NTG):
                        b = nextbank()
                        mm(b, [(wdv[:, c, d * 128:(d + 1) * 128], actb[:, c, tgs(tg)]) for c in range(4)],
                           reads=(("wd", sd),) + tuple(("actb", g % 2, c, tg) for c in range(4)))
                        dve(lambda e, b=b, d=d, tg=tg: e.scalar_tensor_tensor(out=Xv[:, d, tgs(tg)], in0=banks[b][:, 0:TG], scalar=0.5, in1=Xv[:, d, tgs(tg)], op0=ALU.mult, op1=ALU.add),
                            (("ps", b), Xk(d, tg)), (Xk(d, tg),))

        xc = [0]

        cc_ds = S.dsem(newsem("s_cc"))
        XF = 768

        def exchange(tail_ap, past_ap, F, tail_keys, past_key):
            i = xc[0]
            xc[0] += 1
            src = nc.dram_tensor(f"xsrc{i}", [1, 128 * XF], F32).ap()
            gat = nc.dram_tensor(f"xgat{i}", [2, 128 * XF], F32).ap()
            sp_dma(src[0, :].rearrange("(p f) -> p f", p=128)[:, 0:F], tail_ap, reads=tail_keys, writes=(("xsrc", i),))
            S.op("pool", lambda e: e.collective_compute("AllGather", ALU.bypass, replica_groups=groups, ins=[src], outs=[gat]),
                 reads=(("xsrc", i),), writes=(("xgat", i),), dsem=cc_ds, incval=1)
            sp_dma(past_ap, gat[0, :].rearrange("(p f) -> p f", p=128)[:, 0:F], reads=(("xgat", i),), writes=(past_key,))
            dve(lambda e: e.tensor_scalar(out=past_ap, in0=past_ap, scalar1=FLAG, scalar2=None, op0=ALU.mult), (past_key, "cst"), (past_key,))

        misc = {n: S.dsem(newsem('s_m' + n)) for n in ('pwt', 'lw', 'kct', 'vcd')}
        WIN_OFF = [W0 + 17216, W0 + 17216 + 2048]
        st_win = Stream("win", WIN_OFF)
        GATE_OFF = [W0 + 8448, W0 + 8448 + 3072]
        st_gate = Stream("gate", GATE_OFF)
        BRv = [bv(X0 + i * 4224, 4224).rearrange("p (c t) -> p c t", c=8) for i in range(3)]
        XT0 = X0 + 12672
        MERGED = bv(W0, 8448).rearrange("p (k t) -> p k t", k=KC)

        def win_block(l, c0):
            src = d_win[l, :, c0:c0 + 256].rearrange("(k p) c -> p k c", p=128)
            return st_win.load([(lambda off: bv(off, 2048).rearrange("p (k c) -> p k c", k=KC), src)])

        def winv(s):
            return bv(WIN_OFF[s], 2048).rearrange("p (k c) -> p k c", k=KC)

        def proj_blocks(l, col0, nblk, consume):
            slots = {0: win_block(l, col0)}
            for b in range(nblk):
                if b + 1 < nblk:
                    slots[b + 1] = win_block(l, col0 + (b + 1) * 256)
                s = slots[b]
                wv = winv(s)
                for ci in range(2):
                    for tg in range(NTG):
                        bi = nextbank()
                        mm(bi, [(wv[:, k, ci * 128:(ci + 1) * 128], Hv[:, k, tgs(tg)]) for k in range(KC)], reads=(("win", s), ("H",)))
                        consume(b * 2 + ci, tg, bi)

        import os as _os
        MSTOP = int(_os.environ.get('MSTOP', '9'))

        def mixer(l):
            WS = W0
            UEW = 1136
            UE = [fv(WS + c * UEW, UEW) for c in range(8)]
            UEs = [UE[c][:, 1039:1039 + 92].rearrange("p (s j) -> p s j", s=NSQ) for c in range(8)]
            PT = WS + 8 * UEW
            S1 = fv(PT, UEW)
            S2 = fv(PT + UEW, UEW)
            Dbf = bv(PT + 2 * UEW, 1056).rearrange("p (c t) -> p c t", c=2)
            TAILA = fv(XT0, 120).rearrange("p (c j) -> p c j", c=8)
            PASTA = fv(XT0 + 120, 120).rearrange("p (c j) -> p c j", c=8)
            PWT = bv(XT0 + 240, 1024).rearrange("p (g c d) -> p g c d", g=4, c=2)
            S.op("pool", lambda e: e.dma_start(out=PWT, in_=d_poolw[:, l * 2048:(l + 1) * 2048].rearrange("p (g c d) -> p g c d", g=4, c=2)),
                 writes=("pwt",), dsem=misc["pwt"])
            sp_dma(fv(XT0 + 1264, 480).rearrange("p (c s j) -> p c s j", c=8, s=NSQ), d_pool_s[:, l * 480:(l + 1) * 480].rearrange("p (c s j) -> p c s j", c=8, s=NSQ), writes=("pools",))
            POOLS = fv(XT0 + 1264, 480).rearrange("p (c s j) -> p c s j", c=8, s=NSQ)

            def cons_u(c, tg, bi):
                if tg < 2:
                    act(lambda e: e.activation(out=UE[c][:, 15 + tg * TG:15 + (tg + 1) * TG], in_=banks[bi][:, 0:TG], func=AF.Copy), (("ps", bi),), (("ue", c),))
                else:
                    act(lambda e: e.activation(out=UE[c][:, 15 + 704:15 + 1024], in_=banks[bi][:, 0:320], func=AF.Copy), (("ps", bi),), (("ue", c),))
                    act(lambda e: e.activation(out=UEs[c][:, :, 15:23], in_=banks[bi][:, 320:352].rearrange("p (s j) -> p s j", s=NSQ), func=AF.Copy), (("ps", bi),), (("ue", c),))
                    dve(lambda e: e.tensor_copy(out=TAILA[:, c, :], in_=UE[c][:, 1024:1039]), (("ue", c),), ("taila",))
                    dve(lambda e: e.tensor_copy(out=UEs[c][:, :, 0:15], in_=POOLS[:, c, :, :]), ("pools",), (("ue", c),))
            proj_blocks(l, C_POOL, 4, cons_u)
            exchange(fv(XT0, 120), fv(XT0 + 120, 120), 120, ("taila",), "pasta")
            sp_dma(o_pool_p[:, l * 120:(l + 1) * 120], fv(XT0, 120), reads=("taila",))
            sp_dma(o_pool_so[l], d_pool_nat[l, :, 8:15, :])
            for g in range(4):
                w = 2 << g
                for cc in range(2):
                    c = g * 2 + cc
                    dve(lambda e: e.tensor_copy(out=UE[c][:, 0:15], in_=PASTA[:, c, :]), ("pasta",), (("ue", c),))
                    sp_dma(o_pool_sn[:, (l * 8 + c) * TS:(l * 8 + c + 1) * TS].rearrange("p (s j) -> p s j", s=NSQ), UEs[c][:, :, 15:23], reads=(("ue", c),))
                    cur = UE[c]
                    sh = 1
                    bufs = [S1, S2]
                    bi_ = 0
                    while sh < w:
                        dst = bufs[bi_ % 2]
                        dve(lambda e, cur=cur, dst=dst, sh=sh: e.tensor_tensor(out=dst[:, sh:1131], in0=cur[:, sh:1131], in1=cur[:, 0:1131 - sh], op=ALU.add),
                            (("ue", c), ("ptmp", 0), ("ptmp", 1)), (("ptmp", bi_ % 2),))
                        cur = dst
                        sh *= 2
                        bi_ += 1
                    dve(lambda e, cur=cur: e.scalar_tensor_tensor(out=Dbf[:, cc, 16:TP], in0=cur[:, 31:15 + TP], scalar=1.0 / w, in1=UE[c][:, 31:15 + TP], op0=ALU.mult, op1=ALU.subtract),
                        (("ue", c), ("ptmp", 0), ("ptmp", 1)), (("dbf", cc),))
                    dve(lambda e, cur=cur: e.tensor_tensor(out=S2[:, 0:16] if cur is not S2 else S1[:, 0:16], in0=cur[:, 15:31], in1=CST[:, CS_INV + g * 16:CS_INV + (g + 1) * 16], op=ALU.mult),
                        (("ue", c), ("ptmp", 0), ("ptmp", 1), "cst"), (("ptmp", 0), ("ptmp", 1)))
                    tmp16 = (S2 if cur is not S2 else S1)[:, 0:16]
                    dve(lambda e, tmp16=tmp16: e.tensor_tensor(out=Dbf[:, cc, 0:16], in0=tmp16, in1=UE[c][:, 15:31], op=ALU.subtract),
                        (("ue", c), ("ptmp", 0), ("ptmp", 1)), (("dbf", cc),))
                    curs = cur[:, 1039:1039 + 92].rearrange("p (s j) -> p s j", s=NSQ)
                    dve(lambda e, curs=curs: e.scalar_tensor_tensor(out=Dbf[:, cc, TP:T].rearrange("p (s j) -> p s j", s=NSQ), in0=curs[:, :, 15:23], scalar=1.0 / w, in1=UEs[c][:, :, 15:23], op0=ALU.mult, op1=ALU.subtract),
                        (("ue", c), ("ptmp", 0), ("ptmp", 1)), (("dbf", cc),))
                for dd in range(2):
                    for tg in range(NTG):
                        bi = nextbank()
                        mm(bi, [(PWT[:, g, cc, dd * 128:(dd + 1) * 128], Dbf[:, cc, tgs(tg)]) for cc in range(2)], reads=("pwt", ("dbf", 0), ("dbf", 1)))
                        oc = g * 2 + dd
                        act(lambda e, oc=oc, tg=tg, bi=bi: e.activation(out=BRv[0][:, oc, tgs(tg)], in_=banks[bi][:, 0:TG], func=AF.Copy, scale=vec(l, V_PSC + oc)),
                            (("ps", bi), "vec"), (("br", 0),))

            if MSTOP < 2:
                return
            barrier_all()
            XS2 = XT0 + 1744
            XBW = 1072
            XB = [fv(WS + c * XBW, XBW) for c in range(8)]
            XBs = [XB[c][:, 1027:1027 + 44].rearrange("p (s j) -> p s j", s=NSQ) for c in range(8)]
            LT = WS + 8 * XBW
            TAILB = fv(XS2, 24).rearrange("p (c j) -> p c j", c=8)
            PASTB = fv(XS2 + 24, 24).rearrange("p (c j) -> p c j", c=8)
            CONVS = fv(XS2 + 48, 96).rearrange("p (c s j) -> p c s j", c=8, s=NSQ)
            HS = fv(XS2 + 144, 32).rearrange("p (c s) -> p c s", c=8)
            HLOC = fv(XS2 + 176, 8)
            ALAST = fv(XS2 + 184, 8)
            HIN = fv(XS2 + 192, 8)
            HFIN = fv(XS2 + 200, 8)
            LW = bv(XS2 + 208, 1024).rearrange("p (a c m) -> p a c m", a=2, c=8)
            S.op("pool", lambda e: e.dma_start(out=LW, in_=d_lruw[:, l * 2048:(l + 1) * 2048].rearrange("p (a c m) -> p a c m", a=2, c=8)),
                 writes=("lw",), dsem=misc["lw"])
            sp_dma(CONVS, d_conv_s[:, l * 96:(l + 1) * 96].rearrange("p (c s j) -> p c s j", c=8, s=NSQ), writes=("convs",))
            sp_dma(HS, d_h_s[:, l * 32:(l + 1) * 32].rearrange("p (c s) -> p c s", c=8), writes=("hs",))
            SCA = fv(XS2 + 1232, 8)
            SC2 = fv(XS2 + 1240, 8)
            act(lambda e: e.activation(out=SCA, in_=vec(l, V_LAM, 8), func=AF.Exp, scale=-1.0), ("vec",), ("sca",))
            act(lambda e: e.activation(out=SCA, in_=SCA, func=AF.Ln, bias=1.0), ("sca",), ("sca",))
            dve(lambda e: e.tensor_scalar(out=SC2, in0=SCA, scalar1=-16.0, scalar2=None, op0=ALU.mult), ("sca",), ("sc2",))
            dve(lambda e: e.tensor_scalar(out=SCA, in0=SCA, scalar1=-8.0, scalar2=None, op0=ALU.mult), ("sca", "sc2"), ("sca",))

            def cons_xb(c, tg, bi):
                if tg < 2:
                    act(lambda e: e.activation(out=XB[c][:, 3 + tg * TG:3 + (tg + 1) * TG], in_=banks[bi][:, 0:TG], func=AF.Copy), (("ps", bi),), (("xb", c),))
                else:
                    act(lambda e: e.activation(out=XB[c][:, 3 + 704:3 + 1024], in_=banks[bi][:, 0:320], func=AF.Copy), (("ps", bi),), (("xb", c),))
                    act(lambda e: e.activation(out=XBs[c][:, :, 3:11], in_=banks[bi][:, 320:352].rearrange("p (s j) -> p s j", s=NSQ), func=AF.Copy), (("ps", bi),), (("xb", c),))
                    dve(lambda e: e.tensor_copy(out=TAILB[:, c, :], in_=XB[c][:, 1024:1027]), (("xb", c),), ("tailb",))
                    dve(lambda e: e.tensor_copy(out=XBs[c][:, :, 0:3], in_=CONVS[:, c, :, :]), ("convs",), (("xb", c),))
            proj_blocks(l, C_XL, 4, cons_xb)
            exchange(fv(XS2, 24), fv(XS2 + 24, 24), 24, ("tailb",), "pastb")
            sp_dma(o_conv_p[:, l * 24:(l + 1) * 24], fv(XS2, 24), reads=("tailb",))
            XC = fv(LT, T)
            XCB = bv(LT + 1056, 528)
            RR = fv(LT + 1584, T)
            II = fv(LT + 2640, T)
            AA = fv(LT + 3696, T)
            BB = fv(LT + 4752, T)
            GG = fv(LT + 5808, T)
            HH = fv(LT + 6864, T)
            GT = fv(LT + 7920, TG)
            Y2 = BRv[1]

            for c in range(8):
                sp_dma(o_xb_s[:, (l * 8 + c) * TS:(l * 8 + c + 1) * TS].rearrange("p (s j) -> p s j", s=NSQ), XBs[c][:, :, 3:11], reads=(("xb", c),))
                dve(lambda e: e.tensor_copy(out=XB[c][:, 0:3], in_=PASTB[:, c, :]), ("pastb",), (("xb", c),))
                xbk = ("xb", c)
                cw = lambda j: vec(l, V_CW + c * 4 + j)
                XCs = XC[:, TP:T].rearrange("p (s j) -> p s j", s=NSQ)
                dve(lambda e: e.tensor_scalar(out=XC[:, 0:TP], in0=XB[c][:, 3:3 + TP], scalar1=cw(3), scalar2=vec(l, V_CB + c), op0=ALU.mult, op1=ALU.add), (xbk, "vec"), ("xc",))
                dve(lambda e: e.tensor_scalar(out=XCs, in0=XBs[c][:, :, 3:11], scalar1=cw(3), scalar2=vec(l, V_CB + c), op0=ALU.mult, op1=ALU.add), (xbk, "vec"), ("xc",))
                for j in range(3):
                    dve(lambda e, j=j: e.scalar_tensor_tensor(out=XC[:, 0:TP], in0=XB[c][:, j:j + TP], scalar=cw(j), in1=XC[:, 0:TP], op0=ALU.mult, op1=ALU.add), (xbk, "vec", "xc"), ("xc",))
                    dve(lambda e, j=j: e.scalar_tensor_tensor(out=XCs, in0=XBs[c][:, :, j:j + 8], scalar=cw(j), in1=XCs, op0=ALU.mult, op1=ALU.add), (xbk, "vec", "xc"), ("xc",))
                act(lambda e: e.activation(out=XCB, in_=XC, func=AF.Copy), ("xc",), ("xcb",))
                for tg in range(NTG):
                    ba_, bx_ = nextbank(), nextbank()
                    mm(ba_, [(LW[:, 0, c, :], XCB[:, tgs(tg)])], reads=("lw", "xcb"))
                    mm(bx_, [(LW[:, 1, c, :], XCB[:, tgs(tg)])], reads=("lw", "xcb"))
                    act(lambda e, tg=tg, ba_=ba_: e.activation(out=RR[:, tgs(tg)], in_=banks[ba_][:, 0:TG], func=AF.Sigmoid, bias=vec(l, V_BA + c)), (("ps", ba_), "vec"), (("rr", tg),))
                    act(lambda e, tg=tg, bx_=bx_: e.activation(out=II[:, tgs(tg)], in_=banks[bx_][:, 0:TG], func=AF.Sigmoid, bias=vec(l, V_BX + c)), (("ps", bx_), "vec"), (("ii", tg),))
                rrk = [("rr", tg) for tg in range(NTG)]
                iik = [("ii", tg) for tg in range(NTG)]
                act(lambda e: e.activation(out=AA, in_=RR, func=AF.Exp, scale=SCA[:, c:c + 1]), rrk + ["sca"], ("aa",))
                act(lambda e: e.activation(out=BB, in_=RR, func=AF.Exp, scale=SC2[:, c:c + 1]), rrk + ["sc2"], ("bb",))
                dve(lambda e: e.tensor_scalar(out=BB, in0=BB, scalar1=-1.0, scalar2=1.0, op0=ALU.mult, op1=ALU.add), ("bb",), ("bb",))
                act(lambda e: e.activation(out=BB, in_=BB, func=AF.Sqrt), ("bb",), ("bb",))
                dve(lambda e: e.tensor_tensor(out=II, in0=II, in1=XC, op=ALU.mult), iik + ["xc"], iik)
                dve(lambda e: e.tensor_tensor(out=BB, in0=BB, in1=II, op=ALU.mult), iik + ["bb"], ("bb",))
                dve(lambda e: e.tensor_tensor_scan(out=HH[:, 0:TP], data0=AA[:, 0:TP], data1=BB[:, 0:TP], initial=0.0, op0=ALU.mult, op1=ALU.add), ("aa", "bb"), ("hh",))
                for s in range(NSQ):
                    dve(lambda e, s=s: e.tensor_tensor_scan(out=HH[:, TP + s * 8:TP + (s + 1) * 8], data0=AA[:, TP + s * 8:TP + (s + 1) * 8], data1=BB[:, TP + s * 8:TP + (s + 1) * 8], initial=HS[:, c, s:s + 1], op0=ALU.mult, op1=ALU.add),
                        ("aa", "bb", "hs", "hh"), ("hh",))
                dve(lambda e: e.memset(II[:, 0:TP], 0.0), iik + ["bb"], iik)
                dve(lambda e: e.tensor_tensor_scan(out=RR[:, 0:TP], data0=AA[:, 0:TP], data1=II[:, 0:TP], initial=1.0, op0=ALU.mult, op1=ALU.add), ["aa"] + iik + rrk, rrk)
                dve(lambda e: e.tensor_copy(out=HLOC[:, c:c + 1], in_=HH[:, TP - 1:TP]), ("hh",), ("hloc",))
                dve(lambda e: e.tensor_copy(out=ALAST[:, c:c + 1], in_=RR[:, TP - 1:TP]), rrk, ("alast",))
                sp_dma(o_h_s[:, (l * 8 + c) * TS:(l * 8 + c + 1) * TS], HH[:, TP:T], reads=("hh",))
                src = d_win[l, :, C_GL + c * 128:C_GL + (c + 1) * 128].rearrange("(k p) c -> p k c", p=128)
                sg = st_win.load([(lambda off: bv(off, 2048).rearrange("p (k c) -> p k c", k=KC)[:, :, 0:128], src)])
                for tg in range(NTG):
                    bi = nextbank()
                    mm(bi, [(winv(sg)[:, k, 0:128], Hv[:, k, tgs(tg)]) for k in range(KC)], reads=(("win", sg), ("H",)))
                    act(lambda e, tg=tg, bi=bi: e.activation(out=GG[:, tgs(tg)], in_=banks[bi][:, 0:TG], func=AF.Copy), (("ps", bi),), (("gg", tg),))
                    act(lambda e, tg=tg, bi=bi: e.activation(out=GT, in_=banks[bi][:, 0:TG], func=AF.Square), (("ps", bi),), ("gt",))
                    dve(lambda e: e.tensor_scalar(out=GT, in0=GT, scalar1=0.044715, scalar2=1.0, op0=ALU.mult, op1=ALU.add), ("gt",), ("gt",))
                    dve(lambda e, tg=tg: e.tensor_tensor(out=GT, in0=GT, in1=GG[:, tgs(tg)], op=ALU.mult), ("gt", ("gg", tg)), ("gt",))
                    act(lambda e: e.activation(out=GT, in_=GT, func=AF.Sigmoid, scale=1.5957691216057308), ("gt",), ("gt",))
                    dve(lambda e, tg=tg: e.tensor_tensor(out=GG[:, tgs(tg)], in0=GG[:, tgs(tg)], in1=GT, op=ALU.mult), ("gt", ("gg", tg)), (("gg", tg),))
                ggk = [("gg", tg) for tg in range(NTG)]
                dve(lambda e: e.tensor_tensor(out=BRv[2][:, c, :], in0=HH, in1=GG, op=ALU.mult), ["hh"] + ggk, (("br", 2, c),))
                dve(lambda e: e.tensor_tensor(out=Y2[:, c, 0:TP], in0=RR[:, 0:TP], in1=GG[:, 0:TP], op=ALU.mult), rrk + ggk, (("y2", c),))
            exchange(HLOC, HIN, 8, ("hloc",), "hin")
            for c in range(8):
                dve(lambda e, c=c: e.scalar_tensor_tensor(out=BRv[2][:, c, 0:TP], in0=Y2[:, c, 0:TP], scalar=HIN[:, c:c + 1], in1=BRv[2][:, c, 0:TP], op0=ALU.mult, op1=ALU.add),
                    (("y2", c), "hin", ("br", 2, c)), (("br", 2, c),))
            dve(lambda e: e.tensor_tensor(out=HFIN, in0=ALAST, in1=HIN, op=ALU.mult), ("alast", "hin"), ("hfin",))
            dve(lambda e: e.tensor_tensor(out=HFIN, in0=HFIN, in1=HLOC, op=ALU.add), ("hfin", "hloc"), ("hfin",))
            sp_dma(o_h_p[:, l * 8:(l + 1) * 8], HFIN, reads=("hfin",))

            if MSTOP < 3:
                return
            barrier_all()
            KDW = 592
            KD = [bv(WS + j * KDW, KDW) for j in range(4)]
            VD = bv(WS + 2368, 2560).rearrange("p (t j r d) -> p t j r d", t=10, j=4, r=2)
            QQ = bv(WS + 4928, 4224).rearrange("p (c t) -> p c t", c=8)
            COS = fv(A_SQ, T)
            SIN = fv(A_RSTD, T)
            AT = WS + 9152
            QG = fv(AT, TG)
            T1 = fv(AT + 352, TG)
            T2 = fv(AT + 704, TG)
            RS = fv(AT + 1056, TG)
            SQB = bv(AT + 1408, 176)
            KF = fv(AT + 1584, TG)
            PT_ = [bv(AT + 1936 + i * 256, 256) for i in range(4)]
            RD = [fv(AT + 2960 + i * 128, 128) for i in range(2)]
            TAILK = fv(AT + 3216, 512).rearrange("p (j t) -> p j t", j=4)
            TAILV = fv(AT + 3728, 256)
            PASTKV = fv(AT + 3984, 768)
            KSN = fv(AT + 4752, 128).rearrange("p (j t) -> p j t", j=4)
            VSN = fv(AT + 4880, 256)
            KCT = bv(AT + 5136, 1024).rearrange("p (s j t) -> p s j t", s=NSQ, j=4)
            VCD = bv(AT + 6160, 1024).rearrange("p (s j d) -> p s j d", s=NSQ, j=4)
            sp_dma(COS, d_cos[:, :], writes=("cos",))
            sp_dma(SIN, d_sin[:, :], writes=("sin",))
            S.op("pool", lambda e: e.dma_start(out=KCT, in_=d_kcT[:, l * 2048:(l + 1) * 2048].rearrange("p (s j t) -> p s j t", s=NSQ, j=4)), writes=("kct",), dsem=misc["kct"])
            S.op("pool", lambda e: e.dma_start(out=VCD, in_=d_vc[:, l * 2048:(l + 1) * 2048].rearrange("p (s j d) -> p s j d", s=NSQ, j=4)), writes=("vcd",), dsem=misc["vcd"])
            sp_dma(o_k_so[l], d_kc_nat[l, :, 8:128, :])
            sp_dma(o_v_so[l], d_vc_nat[l, :, 8:128, :])
            act(lambda e: e.activation(out=ESNK, in_=vec(l, V_SNK, 16), func=AF.Exp), ("vec",), ("esnk",))

            ROTB = bv(AT + 7184, 64)
            QGB = bv(AT + 7248, 176)
            dve(lambda e: e.tensor_copy(out=ROTB, in_=ROTM), ("cst",), ("rotb",))
            S3SUB = int(_os.environ.get('S3SUB', '9'))
            if S3SUB < 1:
                return

            QSET = [dict(QG=QG, T1=T1, T2=T2, RS=RS, SQB=SQB, QGB=QGB),
                    dict(QG=fv(XT0, TG), T1=fv(XT0 + 352, TG), T2=fv(XT0 + 704, TG), RS=fv(XT0 + 1056, TG), SQB=bv(XT0 + 1408, 176), QGB=bv(XT0 + 1584, 176))]
            qkc = [0]

            def qk_finish(bi, tg, gvec, dst_bf, dst_f32=None):
                i = qkc[0] % 2
                qkc[0] += 1
                q = QSET[i]
                QG_, T1_, T2_, RS_, SQB_, QGB_ = q["QG"], q["T1"], q["T2"], q["RS"], q["SQB"], q["QGB"]
                kqg, ksq, kqb, krs, kt1, kt2 = ("qg", i), ("sqb", i), ("qgb", i), ("rs", i), ("t1", i), ("t2", i)
                act(lambda e: e.activation(out=QG_, in_=banks[bi][:, 0:TG], func=AF.Copy, scale=gvec), (("ps", bi), "vec"), (kqg,))
                act(lambda e: e.activation(out=SQB_, in_=banks[bi][:, 0:TG], func=AF.Square), (("ps", bi),), (ksq,))
                act(lambda e: e.activation(out=QGB_, in_=banks[bi][:, 0:TG], func=AF.Copy, scale=gvec), (("ps", bi), "vec"), (kqb,))
                b2, b3 = nextbank(), nextbank()
                mm(b2, [(OBD, SQB_)], reads=("cb", ksq))
                mm(b3, [(ROTB, QGB_)], reads=("rotb", kqb))
                act(lambda e: e.activation(out=RS_, in_=banks[b2][:, 0:TG], func=AF.Sqrt, scale=1.0 / 64, bias=EPSB), (("ps", b2), "epsb"), (krs,))
                dve(lambda e: e.reciprocal(out=RS_, in_=RS_), (krs,), (krs,))
                dve(lambda e: e.tensor_tensor(out=T1_, in0=QG_, in1=COS[:, tgs(tg)], op=ALU.mult), (kqg, "cos"), (kt1,))
                dve(lambda e: e.tensor_tensor(out=T2_, in0=banks[b3][:, 0:TG], in1=SIN[:, tgs(tg)], op=ALU.mult), (("ps", b3), "sin"), (kt2,))
                dve(lambda e: e.tensor_tensor(out=T1_, in0=T1_, in1=T2_, op=ALU.add), (kt1, kt2), (kt1,))
                if dst_f32 is not None:
                    dve(lambda e: e.tensor_tensor(out=dst_f32, in0=T1_, in1=RS_, op=ALU.mult), (kt1, krs), ("kf",))
                    act(lambda e: e.activation(out=dst_bf, in_=dst_f32, func=AF.Copy), ("kf",), ("kq",))
                else:
                    dve(lambda e: e.tensor_tensor(out=dst_bf, in0=T1_, in1=RS_, op=ALU.mult), (kt1, krs), ("kq",))

            for hp in range(2):
                def kdst(off, hp=hp):
                    return bv(off, 2048).rearrange("p (k j r d) -> p k j r d", k=KC, j=2, r=2)
                parts = []
                for jj in range(2):
                    j = hp * 2 + jj
                    src = d_win[l, :, C_K + j * 64:C_K + (j + 1) * 64].rearrange("(k p) c -> p k c", p=128)
                    for r in range(2):
                        parts.append((lambda off, jj=jj, r=r, hp=hp: bv(off, 2048).rearrange("p (k j r d) -> p k j r d", k=KC, j=2, r=2)[:, :, jj, r, :], src))
                sk = st_win.load(parts)
                wk = bv(WIN_OFF[sk], 2048).rearrange("p (k j m) -> p k j m", k=KC, j=2)
                for jj in range(2):
                    j = hp * 2 + jj
                    for tg in range(NTG):
                        bi = nextbank()
                        mm(bi, [(wk[:, k, jj, :], Hv[:, k, tgs(tg)]) for k in range(KC)], reads=(("win", sk), ("H",)))
                        qk_finish(bi, tg, vec(l, V_KN), KD[j][:, 128 + tg * TG:128 + (tg + 1) * TG], dst_f32=KF)
                        if tg == 2:
                            dve(lambda e, j=j: e.tensor_copy(out=TAILK[:, j, :], in_=KF[:, 192:320]), ("kf",), ("tailk",))
                            dve(lambda e, j=j: e.tensor_copy(out=KSN[:, j, :], in_=KF[:, 320:352]), ("kf",), ("ksn",))
            if S3SUB < 3:
                return
            sv = win_block(l, C_V)
            wv_ = winv(sv)
            for tt in range(int(_os.environ.get('VTT', '9'))):
                ntk = 128 if tt < 8 else TS
                bi = nextbank()
                ov = banks[bi][0:ntk, 0:256]
                mm(bi, [(Hv[:, k, tt * 128:tt * 128 + ntk], wv_[:, k, :]) for k in range(KC)], reads=(("win", sv), ("H",)), out_ap=ov)
                ov4 = ov.rearrange("p (j d) -> p j d", j=4)
                act(lambda e, tt=tt, ov4=ov4, ntk=ntk: e.activation(out=VD[0:ntk, tt + 1, :, 0, :], in_=ov4, func=AF.Copy), (("ps", bi),), ("vd",))
                act(lambda e, tt=tt, ov4=ov4, ntk=ntk: e.activation(out=VD[0:ntk, tt + 1, :, 1, :], in_=ov4, func=AF.Copy), (("ps", bi),), ("vd",))
                if tt == 7:
                    act(lambda e, ov=ov: e.activation(out=TAILV, in_=ov, func=AF.Copy), (("ps", bi),), ("tailv",))
                if tt == 8:
                    act(lambda e, ov=ov: e.activation(out=VSN[0:TS, :], in_=ov, func=AF.Copy), (("ps", bi),), ("vsn",))
            if S3SUB < 4:
                return
            exchange(fv(AT + 3216, 768), PASTKV, 768, ("tailk", "tailv"), "pastkv")
            sp_dma(o_k_p[:, l * 512:(l + 1) * 512], fv(AT + 3216, 512), reads=("tailk",))
            sp_dma(o_v_p[:, l * 256:(l + 1) * 256], TAILV, reads=("tailv",))
            sp_dma(o_k_sn[:, l * 128:(l + 1) * 128], fv(AT + 4752, 128), reads=("ksn",))
            sp_dma(o_v_sn[:, l * 256:(l + 1) * 256], VSN[0:TS, :], reads=("vsn",))
            for j in range(4):
                dve(lambda e, j=j: e.tensor_copy(out=KD[j][:, 0:128], in_=PASTKV[:, j * 128:(j + 1) * 128]), ("pastkv",), ("kq",))
            pv4 = PASTKV[:, 512:768].rearrange("p (j d) -> p j d", j=4)
            for r in range(2):
                dve(lambda e, r=r: e.tensor_copy(out=VD[:, 0, :, r, :], in_=pv4), ("pastkv",), ("vd",))
            if int(_os.environ.get('S3STOP', '9')) < 2:
                return
            if not _os.environ.get('NOQBAR'):
                barrier_all()
            def cons_q(c, tg, bi):
                qk_finish(bi, tg, vec(l, V_QN), QQ[:, c, tgs(tg)])
            proj_blocks(l, C_Q, 4, cons_q)

            if int(_os.environ.get('S3STOP', '9')) < 3:
                return
            pti = [0]
            for n in range(8):
                for j in range(4):
                    pts = []
                    for kt in range(2):
                        b2_ = [nextbank(), nextbank()]
                        kcols = slice(n * 128 + kt * 128, n * 128 + kt * 128 + 128)
                        for hf in range(2):
                            prt = slice(hf * 64, hf * 64 + 64)
                            o = banks[b2_[hf]][:, 0:256].rearrange("p (c q) -> p c q", c=2)
                            S.op("pe", lambda e, o=o, prt=prt, kcols=kcols: e.matmul(o, KD[j][prt, kcols], QQ[prt, 2 * j:2 * j + 2, n * 128:(n + 1) * 128], start=True, stop=True),
                                 reads=("kq",), writes=(("ps", b2_[hf]),), inc=True)
                        pt = PT_[pti[0] % 4]
                        pk = ("pt", pti[0] % 4)
                        pti[0] += 1
                        for hf in range(2):
                            act(lambda e, pt=pt, hf=hf, bb=b2_[hf]: e.activation(out=pt[:, hf * 256:(hf + 1) * 256], in_=banks[bb][:, 0:256], func=AF.Exp, scale=0.125), (("ps", b2_[hf]), pk), (pk,))
                        msk = (MP0 if n == 0 else MP) if kt == 0 else MC
                        dve(lambda e, pt=pt, msk=msk: e.tensor_tensor(out=pt, in0=pt, in1=msk, op=ALU.mult), (pk, "cb"), (pk,))
                        pts.append((pt, pk))
                    bo, bd = nextbank(), nextbank()
                    mm(bo, [(VD[:, n + kt, j, :, :], pts[kt][0]) for kt in range(2)], reads=("vd", pts[0][1], pts[1][1]), out_ap=banks[bo][:, 0:512])
                    mm(bd, [(ONES, pts[kt][0]) for kt in range(2)], reads=("cb", pts[0][1], pts[1][1]), out_ap=banks[bd][:, 0:512])
                    for hh in range(4):
                        hf, cq = hh // 2, hh % 2
                        head = 4 * j + 2 * cq + hf
                        prt = slice(hf * 64, hf * 64 + 64)
                        rd = RD[hh % 2]
                        rk = ("rd", hh % 2)
                        dve(lambda e, rd=rd, prt=prt, hh=hh, head=head: e.tensor_scalar(out=rd[prt, :], in0=banks[bd][prt, hh * 128:(hh + 1) * 128], scalar1=ESNK[prt, head:head + 1], scalar2=None, op0=ALU.add),
                            (("ps", bd), "esnk", rk), (rk,))
                        dve(lambda e, rd=rd, prt=prt: e.reciprocal(out=rd[prt, :], in_=rd[prt, :]), (rk,), (rk,))
                        dve(lambda e, rd=rd, prt=prt, hh=hh, cq=cq: e.tensor_tensor(out=BRv[1][prt, 2 * j + cq, n * 128:(n + 1) * 128], in0=banks[bo][prt, hh * 128:(hh + 1) * 128], in1=rd[prt, :], op=ALU.mult),
                            (("ps", bo), rk), (("br", 1),))
            if int(_os.environ.get('S3STOP', '9')) < 4:
                return
            for j in range(4):
                bc = [nextbank(), nextbank()]
                bn = [nextbank(), nextbank()]
                for s in range(NSQ):
                    for hf in range(2):
                        prt = slice(hf * 64, hf * 64 + 64)
                        o = banks[bc[hf]][:, s * 16:(s + 1) * 16].rearrange("p (c q) -> p c q", c=2)
                        S.op("pe", lambda e, o=o, prt=prt, s=s: e.matmul(o, KCT[prt, s, j, :], QQ[prt, 2 * j:2 * j + 2, TP + s * 8:TP + (s + 1) * 8], start=True, stop=True),
                             reads=("kct", "kq"), writes=(("ps", bc[hf]),), inc=True)
                for hf in range(2):
                    prt = slice(hf * 64, hf * 64 + 64)
                    o = banks[bn[hf]][0:TS, 0:64].rearrange("p (c q) -> p c q", c=2)
                    S.op("pe", lambda e, o=o, prt=prt: e.matmul(o, KD[j][prt, 1152:1184], QQ[prt, 2 * j:2 * j + 2, TP:T], start=True, stop=True),
                         reads=("kq",), writes=(("ps", bn[hf]),), inc=True)
                pc = PT_[0][:, 0:128]
                pn = PT_[1][0:TS, 0:128]
                pc4 = pc.rearrange("p (s x) -> p s x", s=NSQ)
                for hf in range(2):
                    act(lambda e, hf=hf: e.activation(out=pc4[:, :, hf * 16:(hf + 1) * 16], in_=banks[bc[hf]][:, 0:64].rearrange("p (s x) -> p s x", s=NSQ), func=AF.Exp, scale=0.125), (("ps", bc[hf]), ("pt", 0)), (("pt", 0),))
                    act(lambda e, hf=hf: e.activation(out=pn[:, hf * 64:(hf + 1) * 64], in_=banks[bn[hf]][0:TS, 0:64], func=AF.Exp, scale=0.125), (("ps", bn[hf]), ("pt", 1)), (("pt", 1),))
                pc3 = pc.rearrange("p (s x) -> p s x", s=NSQ)
                for s in range(NSQ):
                    dve(lambda e, s=s: e.tensor_tensor(out=pc3[:, s, :], in0=pc3[:, s, :], in1=MSC, op=ALU.mult), (("pt", 0), "cb"), (("pt", 0),))
                dve(lambda e: e.tensor_tensor(out=pn, in0=pn, in1=MSN[0:TS, :], op=ALU.mult), (("pt", 1), "cb"), (("pt", 1),))
                bo, bd = nextbank(), nextbank()
                S.op("pe", lambda e: e.matmul(banks[bo][:, 0:128], VD[0:TS, 9, j, :, :], pn, start=True, stop=False), reads=("vd", ("pt", 1)), writes=(("ps", bo),), inc=False)
                S.op("pe", lambda e: e.matmul(banks[bd][:, 0:128], ONES[0:TS, :], pn, start=True, stop=False), reads=("cb", ("pt", 1)), writes=(("ps", bd),), inc=False)
                for s in range(NSQ):
                    oo = banks[bo][:, 0:128].rearrange("p (h q) -> p h q", h=4)[:, :, s * 8:(s + 1) * 8]
                    od = banks[bd][:, 0:128].rearrange("p (h q) -> p h q", h=4)[:, :, s * 8:(s + 1) * 8]
                    rhs = pc3[:, s, :].rearrange("p (h q) -> p h q", h=4)
                    last = s == NSQ - 1
                    S.op("pe", lambda e, oo=oo, rhs=rhs, s=s, last=last: e.matmul(oo, VCD[:, s, j, :], rhs, start=False, stop=last, skip_group_check=True), reads=("vcd", ("pt", 0)), inc=False)
                    S.op("pe", lambda e, od=od, rhs=rhs, last=last: e.matmul(od, ONES, rhs, start=False, stop=last, skip_group_check=True), reads=("cb", ("pt", 0)), inc=last)
                tk = Tok(S.E["pe"]["sem"], S.E["pe"]["cnt"], "pe", S.E["pe"]["sid"])
                S.lastw[("ps", bo)] = tk
                S.lastw[("ps", bd)] = tk
                S.readers.setdefault(("pt", 0), {})[tk.sid] = tk
                S.readers.setdefault(("pt", 1), {})[tk.sid] = tk
                for hh in range(4):
                    hf, cq = hh // 2, hh % 2
                    head = 4 * j + 2 * cq + hf
                    prt = slice(hf * 64, hf * 64 + 64)
                    rd = RD[hh % 2]
                    rk = ("rd", hh % 2)
                    dve(lambda e, rd=rd, prt=prt, hh=hh, head=head: e.tensor_scalar(out=rd[prt, 0:TS], in0=banks[bd][prt, hh * 32:(hh + 1) * 32], scalar1=ESNK[prt, head:head + 1], scalar2=None, op0=ALU.add),
                        (("ps", bd), "esnk", rk), (rk,))
                    dve(lambda e, rd=rd, prt=prt: e.reciprocal(out=rd[prt, 0:TS], in_=rd[prt, 0:TS]), (rk,), (rk,))
                    dve(lambda e, rd=rd, prt=prt, hh=hh, cq=cq: e.tensor_tensor(out=BRv[1][prt, 2 * j + cq, TP:T], in0=banks[bo][prt, hh * 32:(hh + 1) * 32], in1=rd[prt, 0:TS], op=ALU.mult),
                        (("ps", bo), rk), (("br", 1),))

            if MSTOP < 4:
                return
            barrier_all()
            brkeys = [("br", 0), ("br", 1)] + [("br", 2, c) for c in range(8)]

            def load_gate(dch):
                parts = []
                for b in range(3):
                    src = d_win[l, :, C_GATE + b * D + dch * 128:C_GATE + b * D + (dch + 1) * 128].rearrange("(k p) c -> p k c", p=128)
                    parts.append((lambda off, b=b: bv(off, 3072).rearrange("p (b k c) -> p b k c", b=3, k=KC)[:, b, :, :], src))
                return st_gate.load(parts)

            def load_brw(dch):
                parts = []
                for b in range(3):
                    src = d_wbr[b][l, :, dch * 128:(dch + 1) * 128].rearrange("(k p) c -> p k c", p=128)
                    parts.append((lambda off, b=b: bv(off, 2048)[:, 0:3072].rearrange("p (b k c) -> p b k c", b=3, k=8)[:, b, :, :], src))
                return st_win.load(parts)

            gs = {0: load_gate(0)}
            ws = {0: load_brw(0)}
            SGS = [fv(XT0 + 3400, TG), fv(XT0, TG)]
            MTS = [fv(XT0 + 3752, TG), fv(XT0 + 352, TG)]
            sgc = [0]
            mtc = [0]
            for dch in range(KC):
                if dch + 1 < KC:
                    gs[dch + 1] = load_gate(dch + 1)
                    ws[dch + 1] = load_brw(dch + 1)
                gv = bv(GATE_OFF[gs[dch]], 3072).rearrange("p (b k c) -> p b k c", b=3, k=KC)
                bw = bv(WIN_OFF[ws[dch]], 2048)[:, 0:3072].rearrange("p (b k c) -> p b k c", b=3, k=8)
                for tg in range(NTG):
                    mi = mtc[0] % 2
                    mtc[0] += 1
                    MT = MTS[mi]
                    kmt = ("mt", mi)
                    for b in range(3):
                        si = sgc[0] % 2
                        sgc[0] += 1
                        SG = SGS[si]
                        ksg = ("sg", si)
                        bg, bb = nextbank(), nextbank()
                        mm(bg, [(gv[:, b, k, :], Hv[:, k, tgs(tg)]) for k in range(KC)], reads=(("gate", gs[dch]), ("H",)))
                        mm(bb, [(bw[:, b, k, :], BRv[b][:, k, tgs(tg)]) for k in range(8)], reads=(("win", ws[dch]),) + tuple(brkeys))
                        act(lambda e, bg=bg, SG=SG: e.activation(out=SG, in_=banks[bg][:, 0:TG], func=AF.Sigmoid), (("ps", bg),), (ksg,))
                        if b == 0:
                            dve(lambda e, bb=bb, SG=SG, MT=MT: e.tensor_tensor(out=MT, in0=SG, in1=banks[bb][:, 0:TG], op=ALU.mult), (ksg, ("ps", bb)), (kmt,))
                        else:
                            dve(lambda e, bb=bb, SG=SG: e.tensor_tensor(out=SG, in0=SG, in1=banks[bb][:, 0:TG], op=ALU.mult), (ksg, ("ps", bb)), (ksg,))
                            if b == 1:
                                dve(lambda e, SG=SG, MT=MT: e.tensor_tensor(out=MT, in0=MT, in1=SG, op=ALU.add), (ksg, kmt), (kmt,))
                            else:
                                dve(lambda e, dch=dch, tg=tg, SG=SG, MT=MT: e.tensor_tensor(out=MERGED[:, dch, tgs(tg)], in0=MT, in1=SG, op=ALU.add), (ksg, kmt), (("merged",),))

        def outproj(l):
            def wblock(c0):
                src = d_wout[l, :, c0:c0 + 256].rearrange("(k p) c -> p k c", p=128)
                return st_win.load([(lambda off: bv(off, 2048).rearrange("p (k c) -> p k c", k=KC), src)])
            slots = {0: wblock(0)}
            for b in range(8):
                if b + 1 < 8:
                    slots[b + 1] = wblock((b + 1) * 256)
                wv = winv(slots[b])
                for ci in range(2):
                    d = b * 2 + ci
                    for tg in range(NTG):
                        bi = nextbank()
                        mm(bi, [(wv[:, k, ci * 128:(ci + 1) * 128], MERGED[:, k, tgs(tg)]) for k in range(KC)], reads=(("win", slots[b]), ("merged",)))
                        dve(lambda e, bi=bi, d=d, tg=tg: e.tensor_tensor(out=Xv[:, d, tgs(tg)], in0=Xv[:, d, tgs(tg)], in1=banks[bi][:, 0:TG], op=ALU.add),
                            (("ps", bi), Xk(d, tg)), (Xk(d, tg),))

        allX = [Xk(k, tg) for k in range(KC) for tg in range(NTG)]

        import os
        STOP = int(os.environ.get("KSTOP", "999"))
        nph = [0]

        def ph():
            nph[0] += 1
            return nph[0] <= STOP

        for l in range(LN):
            if ph(): rmsnorm(l, V_NFFA)
            if ph() and not os.environ.get('KSKIPFFN'): ffn(l, 0)
            if ph(): rmsnorm(l, V_NMIX)
            if ph():
                barrier_all()
                sp_dma(xspill[:, :], fv(X0, KC * T), reads=allX, writes=("xspill",))
                barrier_all()
                mixer(l)
                barrier_all()
                sp_dma(fv(X0, KC * T), xspill[:, :], reads=("xspill",), writes=allX)
            if ph(): outproj(l)
            if ph(): rmsnorm(l, V_NFFB)
            if ph() and not os.environ.get('KSKIPFFN'): ffn(l, 1)
        sp_dma(o_y[:, :], fv(X0, KC * T), reads=allX)
        barrier_all(engs=("sp",))
    return nc


NL = DEPTH
_CACHE = {}


def _fm(a, nch):
    sh = a.shape[:-1]
    r = a.reshape(sh + (nch, 128))
    nd = r.ndim
    return np.ascontiguousarray(np.transpose(r, (nd - 1,) + tuple(range(nd - 1))))


def _consts(hf):
    cst = np.zeros((128, NCST), np.float32)
    flag = float(hf)
    cst[:, CS_FLAG] = flag
    for g in range(4):
        w = 2 << g
        for t in range(16):
            cst[:, CS_INV + g * 16 + t] = 1.0 / (min(t + 1, w) if hf == 0 else w)
    for m in range(128):
        d = m % 64
        base = m - d
        if d < 8:
            cst[base + d + 8, CS_ROT + m] = -1.0
        elif d < 16:
            cst[base + d - 8, CS_ROT + m] = 1.0
    k = np.arange(128)
    cst[:, CS_OBD:CS_OBD + 128] = (k[:, None] // 64 == k[None, :] // 64).astype(np.float32)
    mp = (k[None, :] <= k[:, None]).astype(np.float32)
    mc = (k[None, :] >= k[:, None]).astype(np.float32)
    cst[:, CS_MP0:CS_MP0 + 128] = mp * flag
    cst[:, CS_MP:CS_MP + 128] = mp
    cst[:, CS_MC:CS_MC + 128] = mc
    cst[:, CS_MSC:CS_MSC + 8] = (k[:, None] >= np.arange(8)[None, :]).astype(np.float32)
    t32 = np.arange(32)
    msn = ((t32[:, None] // 8 == t32[None, :] // 8) & (t32[:, None] % 8 <= t32[None, :] % 8)).astype(np.float32)
    cst[0:32, CS_MSN:CS_MSN + 32] = msn
    pos = np.concatenate([hf * TP + np.arange(TP), PAST + (np.arange(TS) % SQ)]).astype(np.float32)
    inv = (np.float32(500000.0) ** (-np.arange(8, dtype=np.float32) / np.float32(8))).astype(np.float32)
    ang = (pos[:, None] * inv[None, :]).astype(np.float32)
    cosT = np.ones((128, T), np.float32)
    sinT = np.zeros((128, T), np.float32)
    for p in range(128):
        d = p % 64
        if d < 16:
            cosT[p] = np.cos(ang[:, d % 8])
            sinT[p] = np.sin(ang[:, d % 8])
    return cst, cosT, sinT


def kernel(**inputs):
    LN = NL
    f = lambda a: np.ascontiguousarray(np.asarray(a, dtype=np.float32))
    inp = {k: f(v) for k, v in inputs.items()}
    if LN not in _CACHE:
        _CACHE[LN] = build(LN)
    nc = _CACHE[LN]
    W = {k: inp[k][:LN] for k in inp if k not in ("x_prompt", "x_sample")}
    poolw = np.ascontiguousarray(np.transpose(W["pool_w"].reshape(LN, 4, 2, 128, 256), (3, 0, 1, 2, 4))).reshape(128, -1)
    lruw = np.zeros((128, LN, 2, 8, 128), np.float32)
    for a, nm in enumerate(("lru_gate_a_w", "lru_gate_x_w")):
        for blk in range(16):
            c, hb = blk // 2, blk % 2
            lruw[hb * 64:(hb + 1) * 64, :, a, c, hb * 64:(hb + 1) * 64] = np.transpose(W[nm][:, blk], (1, 0, 2))
    lruw = lruw.reshape(128, -1)
    vecs = np.zeros((128, LN, VL), np.float32)
    for l in range(LN):
        vecs[:, l, V_NFFA:V_NFFA + 16] = W["norm_ffa"][l].reshape(16, 128).T
        vecs[:, l, V_NMIX:V_NMIX + 16] = W["norm_mix"][l].reshape(16, 128).T
        vecs[:, l, V_NFFB:V_NFFB + 16] = W["norm_ffb"][l].reshape(16, 128).T
        vecs[:, l, V_PSC:V_PSC + 8] = W["pool_scale"][l].reshape(8, 128).T
        vecs[:, l, V_CW:V_CW + 32] = np.transpose(W["conv_w"][l].reshape(4, 8, 128), (2, 1, 0)).reshape(128, 32)
        vecs[:, l, V_CB:V_CB + 8] = W["conv_b"][l].reshape(8, 128).T
        vecs[:, l, V_BA:V_BA + 8] = W["lru_gate_a_b"][l].reshape(8, 128).T
        vecs[:, l, V_BX:V_BX + 8] = W["lru_gate_x_b"][l].reshape(8, 128).T
        vecs[:, l, V_LAM:V_LAM + 8] = W["lru_lambda"][l].reshape(8, 128).T
        vecs[:, l, V_QN] = np.tile(W["q_norm"][l], 2)
        vecs[:, l, V_KN] = np.tile(W["k_norm"][l], 2)
        vecs[:, l, V_SNK:V_SNK + 16] = W["attn_sinks"][l][None, :]
    vecs = vecs.reshape(128, -1)
    shared = {
        "ffa_w_gu": W["ffa_w_gu"], "ffb_w_gu": W["ffb_w_gu"], "ffa_w_down": W["ffa_w_down"], "ffb_w_down": W["ffb_w_down"],
        "w_in": W["w_in"], "w_branch_pool": W["w_branch_pool"], "w_branch_attn": W["w_branch_attn"], "w_branch_lru": W["w_branch_lru"],
        "w_out": W["w_out"], "poolw": poolw, "lruw": lruw, "vecs": vecs,
    }
    cs = [_consts(0), _consts(1)]
    in_maps = []
    for c in range(NCORES):
        b, hf = c // 2, c % 2
        xa = np.concatenate([inp["x_prompt"][b, hf * TP:(hf + 1) * TP], inp["x_sample"][NSQ * c:NSQ * (c + 1)].reshape(TS, D)], axis=0)
        m = dict(shared)
        m["xT"] = np.ascontiguousarray(np.transpose(xa.reshape(T, KC, 128), (2, 1, 0))).reshape(128, -1)
        m["cst"], m["cosT"], m["sinT"] = cs[hf]
        sl = slice(NSQ * c, NSQ * (c + 1))
        m["pool_s"] = np.ascontiguousarray(np.transpose(W["state_pool"][:, sl].reshape(LN, NSQ, 15, 8, 128), (4, 0, 3, 1, 2))).reshape(128, -1)
        m["conv_s"] = np.ascontiguousarray(np.transpose(W["state_conv"][:, sl].reshape(LN, NSQ, 3, 8, 128), (4, 0, 3, 1, 2))).reshape(128, -1)
        m["h_s"] = np.ascontiguousarray(np.transpose(W["state_rglru"][:, sl].reshape(LN, NSQ, 8, 128), (3, 0, 2, 1))).reshape(128, -1)
        kT = np.transpose(W["cache_k_win"][:, sl], (4, 0, 1, 3, 2))
        m["kcT"] = np.ascontiguousarray(np.concatenate([kT, kT], axis=0)).reshape(128, -1)
        vv = np.transpose(W["cache_v_win"][:, sl], (2, 0, 1, 3, 4))
        m["vc"] = np.ascontiguousarray(np.concatenate([vv, vv], axis=-1)).reshape(128, -1)
        m["kc_nat"] = np.ascontiguousarray(W["cache_k_win"][:, sl].reshape(LN, NSQ, 128, 256))
        m["vc_nat"] = np.ascontiguousarray(W["cache_v_win"][:, sl].reshape(LN, NSQ, 128, 256))
        m["pool_nat"] = np.ascontiguousarray(W["state_pool"][:, sl])
        in_maps.append(m)
    import os as _os1
    for nm in [x for x in _os1.environ.get("KSMALL", "").split(",") if x]:
        for m in in_maps:
            m[nm] = np.zeros((1, 16), np.float32)
    res = run_bass_kernel_spmd(nc, in_maps, core_ids=list(range(NCORES)))
    R = res.results
    B = 4
    y_p = np.zeros((B, 2 * TP, D), np.float32)
    y_s = np.zeros((NCORES * NSQ, SQ, D), np.float32)
    pool_p = np.zeros((LN, B, 15, 1024), np.float32)
    pool_s = np.zeros((LN, 32, 15, 1024), np.float32)
    k_p = np.zeros((LN, B, 128, 4, 64), np.float32)
    k_s = np.zeros((LN, 32, 128, 4, 64), np.float32)
    v_p = np.zeros((LN, B, 128, 4, 64), np.float32)
    v_s = np.zeros((LN, 32, 128, 4, 64), np.float32)
    conv_p = np.zeros((LN, B, 3, 1024), np.float32)
    conv_s = np.zeros((LN, 32, 3, 1024), np.float32)
    h_p = np.zeros((LN, B, 1024), np.float32)
    h_s = np.zeros((LN, 32, 1024), np.float32)
    for c in range(NCORES):
        b, hf = c // 2, c % 2
        r = R[c]
        sl = slice(NSQ * c, NSQ * (c + 1))
        y = np.transpose(np.asarray(r["yT"]).reshape(128, KC, T), (2, 1, 0)).reshape(T, D)
        y_p[b, hf * TP:(hf + 1) * TP] = y[:TP]
        y_s[sl] = y[TP:].reshape(NSQ, SQ, D)
        pool_s[:, sl, 0:7] = np.asarray(r["o_pool_so"])
        pool_s[:, sl, 7:15] = np.transpose(np.asarray(r["o_pool_sn"]).reshape(128, LN, 8, NSQ, SQ), (1, 3, 4, 2, 0)).reshape(LN, NSQ, SQ, 1024)
        k_s[:, sl, 0:120] = np.asarray(r["o_k_so"]).reshape(LN, NSQ, 120, 4, 64)
        k_s[:, sl, 120:128] = np.transpose(np.asarray(r["o_k_sn"]).reshape(128, LN, 4, NSQ, SQ)[0:64], (1, 3, 4, 2, 0))
        v_s[:, sl, 0:120] = np.asarray(r["o_v_so"]).reshape(LN, NSQ, 120, 4, 64)
        v_s[:, sl, 120:128] = np.transpose(np.asarray(r["o_v_sn"]).reshape(NSQ, SQ, LN, 4, 64), (2, 0, 1, 3, 4))
        conv_s[:, sl] = np.transpose(np.asarray(r["o_xb_s"]).reshape(128, LN, 8, NSQ, SQ)[..., 5:8], (1, 3, 4, 2, 0)).reshape(LN, NSQ, 3, 1024)
        h_s[:, sl] = np.transpose(np.asarray(r["o_h_s"]).reshape(128, LN, 8, NSQ, SQ)[..., 7], (1, 3, 2, 0)).reshape(LN, NSQ, 1024)
        if hf == 1:
            pool_p[:, b] = np.transpose(np.asarray(r["o_pool_p"]).reshape(128, LN, 8, 15), (1, 3, 2, 0)).reshape(LN, 15, 1024)
            k_p[:, b] = np.transpose(np.asarray(r["o_k_p"]).reshape(128, LN, 4, 128)[0:64], (1, 3, 2, 0))
            v_p[:, b] = np.transpose(np.asarray(r["o_v_p"]).reshape(128, LN, 4, 64), (1, 0, 2, 3))
            conv_p[:, b] = np.transpose(np.asarray(r["o_conv_p"]).reshape(128, LN, 8, 3), (1, 3, 2, 0)).reshape(LN, 3, 1024)
            h_p[:, b] = np.transpose(np.asarray(r["o_h_p"]).reshape(128, LN, 8), (1, 2, 0)).reshape(LN, 1024)
    return (y_p, y_s, pool_p, pool_s, k_p, k_s, v_p, v_s, conv_p, conv_s, h_p, h_s)
```

```python
import numpy as np
import concourse.bass as bass
import concourse.mybir as mybir
from concourse.bass_utils import run_bass_kernel_spmd

F32 = mybir.dt.float32
BF16 = mybir.dt.bfloat16
AF = mybir.ActivationFunctionType
ALU = mybir.AluOpType

NCORES = 8
DEPTH = 4
D = 2048
KC = 16
TP = 1024
NSQ = 4
SQ = 8
TS = NSQ * SQ
T = TP + TS
TG = 352
NTG = 3
DFF = 5632
NFC = DFF // 128
INC = 10752
C_POOL, C_Q, C_K, C_V, C_XL, C_GL, C_GATE = 0, 1024, 2048, 2304, 2560, 3584, 4608
PAST = 16384
EPS = 1e-6
VL = 138
V_NFFA, V_NMIX, V_NFFB, V_PSC, V_CW, V_CB, V_BA, V_BX, V_LAM, V_QN, V_KN, V_SNK = 0, 16, 32, 48, 56, 88, 96, 104, 112, 120, 121, 122
CS_FLAG, CS_INV, CS_ROT, CS_OBD, CS_MP0, CS_MP, CS_MC, CS_MSC, CS_MSN, NCST = 0, 2, 66, 194, 322, 450, 578, 706, 714, 748

X0 = 0
H0 = 16896
W0 = 25344
WSZ = 21312
P0 = W0 + WSZ
A_SQ = P0
A_RSTD = A_SQ + 1056
A_VEC = A_RSTD + 1056
A_CST = A_VEC + DEPTH * VL
A_CB = A_CST + NCST
CB_ONES, CB_OBD, CB_MP0, CB_MP, CB_MC, CB_MSC, CB_MSN = 0, 64, 128, 384, 640, 896, 912
A_ESNK = A_CB + 976
A_SP = A_ESNK + 16
A_END = A_SP + 16
NW = A_END


class Tok:
    __slots__ = ("sem", "val", "eng", "sid")

    def __init__(self, sem, val, eng, sid):
        self.sem, self.val, self.eng, self.sid = sem, val, eng, sid


class Sched:
    def __init__(self, nc):
        self.nc = nc
        self.E = {}
        self.lastw = {}
        self.readers = {}
        self.nsem = 0
        self.all_dsems = []

    def add_engine(self, name, eng, sem):
        self.E[name] = dict(eng=eng, sem=sem, cnt=0, waited={}, sid=self._sid())

    def _sid(self):
        self.nsem += 1
        return self.nsem

    def dsem(self, sem):
        d = dict(sem=sem, cnt=0, sid=self._sid())
        self.all_dsems.append(d)
        return d

    def op(self, ename, fn, reads=(), writes=(), dsem=None, inc=True, incval=16):
        E = self.E[ename]
        deps = []
        for k in reads:
            t = self.lastw.get(k)
            if t is not None:
                deps.append(t)
            if isinstance(k, tuple) and k[0] == "ps":
                r = self.readers.get(k)
                if r:
                    deps.extend(x for x in r.values() if x.eng != ename)
        for k in writes:
            t = self.lastw.get(k)
            if t is not None:
                deps.append(t)
            r = self.readers.get(k)
            if r:
                deps.extend(r.values())
        for t in deps:
            if t.eng == ename and ename == "pe":
                continue
            if E["waited"].get(t.sid, 0) >= t.val:
                continue
            E["eng"].wait_ge(t.sem, t.val)
            E["waited"][t.sid] = t.val
        inst = fn(E["eng"])
        tok = None
        if dsem is not None:
            dsem["cnt"] += incval
            inst.then_inc(dsem["sem"], incval)
            tok = Tok(dsem["sem"], dsem["cnt"], "dma", dsem["sid"])
        elif inc:
            E["cnt"] += 1
            inst.then_inc(E["sem"], 1)
            tok = Tok(E["sem"], E["cnt"], ename, E["sid"])
        if tok is not None:
            for k in reads:
                self.readers.setdefault(k, {})[tok.sid] = tok
            for k in writes:
                self.lastw[k] = tok
                self.readers[k] = {}
        return tok


def build(nlayers=DEPTH):
    nc = bass.Bass("TRN2", target_bir_lowering=False)
    LN = nlayers

    import os as _os0
    SMALL = set(x for x in _os0.environ.get("KSMALL", "").split(",") if x)

    def din(name, shape):
        if name in SMALL:
            shape = [1, 16]
        return nc.dram_tensor(name, list(shape), F32, kind="ExternalInput").ap()

    def dout(name, shape):
        return nc.dram_tensor(name, list(shape), F32, kind="ExternalOutput").ap()

    d_x = din("xT", [128, KC * T])
    d_wgu = [din("ffa_w_gu", [LN, D, 2 * DFF]), din("ffb_w_gu", [LN, D, 2 * DFF])]
    d_wd = [din("ffa_w_down", [LN, DFF, D]), din("ffb_w_down", [LN, DFF, D])]
    d_win = din("w_in", [LN, D, INC])
    d_wbr = [din("w_branch_pool", [LN, 1024, D]), din("w_branch_attn", [LN, 1024, D]), din("w_branch_lru", [LN, 1024, D])]
    d_wout = din("w_out", [LN, D, D])
    d_poolw = din("poolw", [128, LN * 4 * 2 * 256])
    d_lruw = din("lruw", [128, LN * 2 * 8 * 128])
    d_vecs = din("vecs", [128, LN * VL])
    d_cst = din("cst", [128, NCST])
    d_cos = din("cosT", [128, T])
    d_sin = din("sinT", [128, T])
    d_pool_s = din("pool_s", [128, LN * 8 * NSQ * 15])
    d_conv_s = din("conv_s", [128, LN * 8 * NSQ * 3])
    d_h_s = din("h_s", [128, LN * 8 * NSQ])
    d_kcT = din("kcT", [128, LN * NSQ * 4 * 128])
    d_vc = din("vc", [128, LN * NSQ * 4 * 128])
    d_kc_nat = din("kc_nat", [LN, NSQ, 128, 256])
    d_vc_nat = din("vc_nat", [LN, NSQ, 128, 256])
    d_pool_nat = din("pool_nat", [LN, NSQ, 15, 1024])

    o_y = dout("yT", [128, KC * T])
    o_pool_p = dout("o_pool_p", [128, LN * 8 * 15])
    o_pool_sn = dout("o_pool_sn", [128, LN * 8 * TS])
    o_pool_so = dout("o_pool_so", [LN, NSQ, 7, 1024])
    o_k_p = dout("o_k_p", [128, LN * 4 * 128])
    o_k_sn = dout("o_k_sn", [128, LN * 4 * TS])
    o_k_so = dout("o_k_so", [LN, NSQ, 120, 256])
    o_v_p = dout("o_v_p", [128, LN * 256])
    o_v_sn = dout("o_v_sn", [TS, LN * 256])
    o_v_so = dout("o_v_so", [LN, NSQ, 120, 256])
    o_conv_p = dout("o_conv_p", [128, LN * 8 * 3])
    o_xb_s = dout("o_xb_s", [128, LN * 8 * TS])
    o_h_p = dout("o_h_p", [128, LN * 8])
    o_h_s = dout("o_h_s", [128, LN * 8 * TS])

    xspill = nc.dram_tensor("xspill", [128, KC * T], F32).ap()
    groups = [[0, 1], [2, 3], [4, 5], [6, 7]]

    import contextlib
    with contextlib.ExitStack() as es:
        arena = es.enter_context(nc.sbuf_tensor("arena", [128, NW], F32))
        banks = [es.enter_context(nc.psum_tensor(f"bank{i}", [128, 512], F32)) for i in range(8)]
        sems = {}

        def newsem(name):
            s = es.enter_context(nc.semaphore(name))
            sems[name] = s
            return s

        S = Sched(nc)
        all_dsems = S.all_dsems
        block = es.enter_context(nc.Block())
        S.add_engine("pe", nc.tensor, newsem("s_pe"))
        S.add_engine("act", nc.scalar, newsem("s_act"))
        S.add_engine("dve", nc.vector, newsem("s_dve"))
        S.add_engine("pool", nc.gpsimd, newsem("s_pool"))
        S.add_engine("sp", nc.sync, newsem("s_sp"))

        def fv(off, n):
            return arena[:, off:off + n]

        def bv(off, nwords):
            return arena[:, off:off + nwords].bitcast(BF16)

        Xv = fv(X0, KC * T).rearrange("p (k t) -> p k t", k=KC)
        Hv = bv(H0, KC * T // 2).rearrange("p (k t) -> p k t", k=KC)
        SQv = [bv(A_SQ + i * 528, 528) for i in range(2)]
        RSTD = fv(A_RSTD, T)
        VEC = fv(A_VEC, DEPTH * VL)
        CST = fv(A_CST, NCST)
        ONES = bv(A_CB + CB_ONES, 64)
        OBD = bv(A_CB + CB_OBD, 64)
        MP0 = bv(A_CB + CB_MP0, 256)
        MP = bv(A_CB + CB_MP, 256)
        MC = bv(A_CB + CB_MC, 256)
        MSC = bv(A_CB + CB_MSC, 16)
        MSN = bv(A_CB + CB_MSN, 64)
        ESNK = fv(A_ESNK, 16)
        SPV = fv(A_SP, 16)
        FLAG = CST[:, CS_FLAG:CS_FLAG + 1]
        ROTM = CST[:, CS_ROT:CS_ROT + 128]

        def vec(l, off, n=1):
            return VEC[:, l * VL + off:l * VL + off + n]

        sp_sems = [S.dsem(newsem(f"s_spd{i}")) for i in range(8)]
        sp_rr = [0]
        sp_last = [None] * 8

        def sp_dma(out, in_, reads=(), writes=()):
            i = sp_rr[0] % 8
            sp_rr[0] += 1
            ds = sp_sems[i]
            E = S.E["sp"]
            if ds["cnt"] > 0 and E["waited"].get(ds["sid"], 0) < ds["cnt"]:
                nc.sync.wait_ge(ds["sem"], ds["cnt"])
                E["waited"][ds["sid"]] = ds["cnt"]
            return S.op("sp", lambda e: e.dma_start(out=out, in_=in_), reads=reads, writes=writes, dsem=ds)

        class Stream:
            def __init__(self, name, offs):
                self.name = name
                self.offs = offs
                self.ds = [S.dsem(newsem(f"s_{name}{i}")) for i in range(len(offs))]
                self.n = 0

            def load(self, parts):
                s = self.n % len(self.offs)
                self.n += 1
                key = (self.name, s)
                for j, (dst, src) in enumerate(parts):
                    S.op("pool", lambda e, dst=dst, src=src: e.dma_start(out=dst(self.offs[s]), in_=src),
                         writes=(key,), dsem=self.ds[s])
                return s

        ring = [0]

        def nextbank():
            i = ring[0] % 8
            ring[0] += 1
            return i

        def mm(bi, pairs, reads, out_ap=None, extra_w=()):
            n = len(pairs)
            o = out_ap if out_ap is not None else banks[bi][:, 0:TG]
            tok = None
            for j, (lt, rh) in enumerate(pairs):
                first, last = j == 0, j == n - 1
                tok = S.op("pe", lambda e, lt=lt, rh=rh, first=first, last=last: e.matmul(o, lt, rh, start=first, stop=last),
                           reads=reads if first else (), writes=((("ps", bi),) + tuple(extra_w)) if first else (), inc=last)
            S.lastw[("ps", bi)] = tok
            S.readers[("ps", bi)] = {}
            for k in reads:
                S.readers.setdefault(k, {})[tok.sid] = tok
            return tok

        def act(fn, reads, writes):
            return S.op("act", fn, reads=reads, writes=writes)

        def dve(fn, reads, writes):
            return S.op("dve", fn, reads=reads, writes=writes)

        def tgs(tg):
            return slice(tg * TG, (tg + 1) * TG)

        Xk = lambda k, tg: ("X", k, tg)


        def barrier_all(engs=("pe", "act", "dve", "pool", "sp")):
            toks = []
            for en in ("pe", "act", "dve"):
                E = S.E[en]
                if E["cnt"] > 0:
                    toks.append(Tok(E["sem"], E["cnt"], en, E["sid"]))
            for ds in all_dsems:
                if ds["cnt"] > 0:
                    toks.append(Tok(ds["sem"], ds["cnt"], "dma", ds["sid"]))
            for en in engs:
                E = S.E[en]
                for t in toks:
                    if E["waited"].get(t.sid, 0) >= t.val:
                        continue
                    E["eng"].wait_ge(t.sem, t.val)
                    E["waited"][t.sid] = t.val

        sp_dma(VEC[:, 0:LN * VL], d_vecs[:, :], writes=("vec",))
        sp_dma(CST, d_cst[:, :], writes=("cst",))
        sp_dma(fv(X0, KC * T), d_x[:, :], writes=[Xk(k, tg) for k in range(KC) for tg in range(NTG)])
        dve(lambda e: e.memset(ONES, 1.0), (), ("cb",))
        dve(lambda e: e.tensor_copy(out=OBD, in_=CST[:, CS_OBD:CS_OBD + 128]), ("cst",), ("cb",))
        for (dst, so) in ((MP0, CS_MP0), (MP, CS_MP), (MC, CS_MC)):
            for r in range(4):
                dve(lambda e, dst=dst, so=so, r=r: e.tensor_copy(out=dst[:, r * 128:(r + 1) * 128], in_=CST[:, so:so + 128]), ("cst",), ("cb",))
        for r in range(4):
            dve(lambda e, r=r: e.tensor_copy(out=MSC[:, r * 8:(r + 1) * 8], in_=CST[:, CS_MSC:CS_MSC + 8]), ("cst",), ("cb",))
            dve(lambda e, r=r: e.tensor_copy(out=MSN[0:32, r * 32:(r + 1) * 32], in_=CST[0:32, CS_MSN:CS_MSN + 32]), ("cst",), ("cb",))

        def rmsnorm(l, goff):
            bs = [nextbank() for _ in range(NTG)]
            for k in range(KC):
                sq = SQv[k % 2]
                act(lambda e, k=k, sq=sq: e.activation(out=sq, in_=Xv[:, k, :], func=AF.Square),
                    [Xk(k, tg) for tg in range(NTG)], (("sq", k % 2),))
                for tg in range(NTG):
                    S.op("pe", lambda e, tg=tg, sq=sq, k=k: e.matmul(banks[bs[tg]][:, 0:TG], ONES, sq[:, tgs(tg)], start=(k == 0), stop=(k == KC - 1)),
                         reads=(("sq", k % 2), "cb"), writes=(("ps", bs[tg]),) if k == 0 else (), inc=True)
                    if k == KC - 1:
                        S.lastw[("ps", bs[tg])] = Tok(S.E["pe"]["sem"], S.E["pe"]["cnt"], "pe", S.E["pe"]["sid"])
            for tg in range(NTG):
                act(lambda e, tg=tg: e.activation(out=RSTD[:, tgs(tg)], in_=banks[bs[tg]][:, 0:TG], func=AF.Sqrt, scale=1.0 / D, bias=EPSB),
                    (("ps", bs[tg]),), (("rstd", tg),))
                dve(lambda e, tg=tg: e.reciprocal(out=RSTD[:, tgs(tg)], in_=RSTD[:, tgs(tg)]), (("rstd", tg),), (("rstd", tg),))
            for k in range(KC):
                dve(lambda e, k=k: e.scalar_tensor_tensor(out=Hv[:, k, :], in0=Xv[:, k, :], scalar=vec(l, goff + k), in1=RSTD, op0=ALU.mult, op1=ALU.mult),
                    [Xk(k, tg) for tg in range(NTG)] + [("rstd", tg) for tg in range(NTG)] + ["vec"], (("H",),))

        EPSB = fv(A_SP + 15, 1)
        dve(lambda e: e.memset(EPSB, EPS), (), ("epsb",))

        WGU_OFF = [W0, W0 + 4096]
        WD_OFF = [W0 + 8192, W0 + 12288]
        ACT_OFF = [W0 + 16384, W0 + 16384 + 2112]
        SIL_OFF = [W0 + 20608, W0 + 20608 + 352]
        st_wgu = Stream("wgu", WGU_OFF)
        st_wd = Stream("wd", WD_OFF)

        def ffn(l, which):
            wgu_d, wd_d = d_wgu[which], d_wd[which]
            barrier_all(engs=("pool",))

            def load_gu(u):
                c0 = u * 256
                gsrc = wgu_d[l, :, c0:c0 + 256].rearrange("(k p) c -> p k c", p=128)
                usrc = wgu_d[l, :, DFF + c0:DFF + c0 + 256].rearrange("(k p) c -> p k c", p=128)
                return st_wgu.load([
                    (lambda off: bv(off, 4096).rearrange("p (k c) -> p k c", k=KC)[:, :, 0:256], gsrc),
                    (lambda off: bv(off, 4096).rearrange("p (k c) -> p k c", k=KC)[:, :, 256:512], usrc)])

            def load_d(g):
                src = wd_d[l, g * 512:(g + 1) * 512, :].rearrange("(c p) n -> p c n", p=128)
                return st_wd.load([(lambda off: bv(off, 4096).rearrange("p (c n) -> p c n", c=4), src)])

            NU = NFC // 2
            NG = NFC // 4
            slots_gu = {0: load_gu(0)}
            slots_d = {0: load_d(0)}
            yield
            sil_i = [0]
            for g in range(NG):
                actb = bv(ACT_OFF[g % 2], 2112).rearrange("p (c t) -> p c t", c=4)
                for half in range(2):
                    u = g * 2 + half
                    if u + 1 < NU:
                        slots_gu[u + 1] = load_gu(u + 1)
                    s = slots_gu[u]
                    wv = bv(WGU_OFF[s], 4096).rearrange("p (k c) -> p k c", k=KC)
                    for ci in range(2):
                        for tg in range(NTG):
                            bg, bu = nextbank(), nextbank()
                            mm(bg, [(wv[:, k, ci * 128:(ci + 1) * 128], Hv[:, k, tgs(tg)]) for k in range(KC)], reads=(("wgu", s), ("H",)))
                            mm(bu, [(wv[:, k, 256 + ci * 128:256 + (ci + 1) * 128], Hv[:, k, tgs(tg)]) for k in range(KC)], reads=(("wgu", s), ("H",)))
                            si = sil_i[0] % 2
                            sil_i[0] += 1
                            sil = fv(SIL_OFF[si], TG)
                            act(lambda e, sil=sil, bg=bg: e.activation(out=sil, in_=banks[bg][:, 0:TG], func=AF.Silu), (("ps", bg),), (("sil", si),))
                            cc = half * 2 + ci
                            dve(lambda e, sil=sil, bu=bu, cc=cc, tg=tg, actb=actb: e.tensor_tensor(out=actb[:, cc, tgs(tg)], in0=sil, in1=banks[bu][:, 0:TG], op=ALU.mult),
                                (("sil", si), ("ps", bu)), (("actb", g % 2, cc, tg),))
                if g + 1 < NG:
                    slots_d[g + 1] = load_d(g + 1)
                sd = slots_d[g]
                wdv = bv(WD_OFF[sd], 4096).rearrange("p (c n) -> p c n", c=4)
                for d in range(KC):
                    for tg in range(NTG):
                        b = nextbank()
                        mm(b, [(wdv[:, c, d * 128:(d + 1) * 128], actb[:, c, tgs(tg)]) for c in range(4)],
                           reads=(("wd", sd),) + tuple(("actb", g % 2, c, tg) for c in range(4)))
                        dve(lambda e, b=b, d=d, tg=tg: e.scalar_tensor_tensor(out=Xv[:, d, tgs(tg)], in0=banks[b][:, 0:TG], scalar=0.5, in1=Xv[:, d, tgs(tg)], op0=ALU.mult, op1=ALU.add),
                            (("ps", b), Xk(d, tg)), (Xk(d, tg),))

        xc = [0]

        cc_ds = S.dsem(newsem("s_cc"))
        XF = 768

        def exchange(tail_ap, past_ap, F, tail_keys, past_key):
            i = xc[0]
            xc[0] += 1
            src = nc.dram_tensor(f"xsrc{i}", [1, 128 * XF], F32).ap()
            gat = nc.dram_tensor(f"xgat{i}", [2, 128 * XF], F32).ap()
            sp_dma(src[0, :].rearrange("(p f) -> p f", p=128)[:, 0:F], tail_ap, reads=tail_keys, writes=(("xsrc", i),))
            S.op("pool", lambda e: e.collective_compute("AllGather", ALU.bypass, replica_groups=groups, ins=[src], outs=[gat]),
                 reads=(("xsrc", i),), writes=(("xgat", i),), dsem=cc_ds, incval=1)
            sp_dma(past_ap, gat[0, :].rearrange("(p f) -> p f", p=128)[:, 0:F], reads=(("xgat", i),), writes=(past_key,))
            dve(lambda e: e.tensor_scalar(out=past_ap, in0=past_ap, scalar1=FLAG, scalar2=None, op0=ALU.mult), (past_key, "cst"), (past_key,))

        misc = {n: S.dsem(newsem('s_m' + n)) for n in ('pwt', 'lw', 'kct', 'vcd')}
        WIN_OFF = [W0 + 17216, W0 + 17216 + 2048]
        st_win = Stream("win", WIN_OFF)
        GATE_OFF = [W0 + 8448, W0 + 8448 + 3072]
        st_gate = Stream("gate", GATE_OFF)
        BRv = [bv(X0 + i * 4224, 4224).rearrange("p (c t) -> p c t", c=8) for i in range(3)]
        XT0 = X0 + 12672
        MERGED = bv(W0, 8448).rearrange("p (k t) -> p k t", k=KC)

        def win_block(l, c0):
            src = d_win[l, :, c0:c0 + 256].rearrange("(k p) c -> p k c", p=128)
            return st_win.load([(lambda off: bv(off, 2048).rearrange("p (k c) -> p k c", k=KC), src)])

        def winv(s):
            return bv(WIN_OFF[s], 2048).rearrange("p (k c) -> p k c", k=KC)

        def proj_blocks(l, col0, nblk, consume):
            slots = {0: win_block(l, col0)}
            for b in range(nblk):
                if b + 1 < nblk:
                    slots[b + 1] = win_block(l, col0 + (b + 1) * 256)
                s = slots[b]
                wv = winv(s)
                for ci in range(2):
                    for tg in range(NTG):
                        bi = nextbank()
                        mm(bi, [(wv[:, k, ci * 128:(ci + 1) * 128], Hv[:, k, tgs(tg)]) for k in range(KC)], reads=(("win", s), ("H",)))
                        consume(b * 2 + ci, tg, bi)

        import os as _os
        MSTOP = int(_os.environ.get('MSTOP', '9'))

        def mixer(l):
            WS = W0
            UEW = 1136
            UE = [fv(WS + c * UEW, UEW) for c in range(8)]
            UEs = [UE[c][:, 1039:1039 + 92].rearrange("p (s j) -> p s j", s=NSQ) for c in range(8)]
            PT = WS + 8 * UEW
            S1 = fv(PT, UEW)
            S2 = fv(PT + UEW, UEW)
            Dbf = bv(PT + 2 * UEW, 1056).rearrange("p (c t) -> p c t", c=2)
            TAILA = fv(XT0, 120).rearrange("p (c j) -> p c j", c=8)
            PASTA = fv(XT0 + 120, 120).rearrange("p (c j) -> p c j", c=8)
            PWT = bv(XT0 + 240, 1024).rearrange("p (g c d) -> p g c d", g=4, c=2)
            S.op("pool", lambda e: e.dma_start(out=PWT, in_=d_poolw[:, l * 2048:(l + 1) * 2048].rearrange("p (g c d) -> p g c d", g=4, c=2)),
                 writes=("pwt",), dsem=misc["pwt"])
            sp_dma(fv(XT0 + 1264, 480).rearrange("p (c s j) -> p c s j", c=8, s=NSQ), d_pool_s[:, l * 480:(l + 1) * 480].rearrange("p (c s j) -> p c s j", c=8, s=NSQ), writes=("pools",))
            POOLS = fv(XT0 + 1264, 480).rearrange("p (c s j) -> p c s j", c=8, s=NSQ)

            def cons_u(c, tg, bi):
                if tg < 2:
                    act(lambda e: e.activation(out=UE[c][:, 15 + tg * TG:15 + (tg + 1) * TG], in_=banks[bi][:, 0:TG], func=AF.Copy), (("ps", bi),), (("ue", c),))
                else:
                    act(lambda e: e.activation(out=UE[c][:, 15 + 704:15 + 1024], in_=banks[bi][:, 0:320], func=AF.Copy), (("ps", bi),), (("ue", c),))
                    act(lambda e: e.activation(out=UEs[c][:, :, 15:23], in_=banks[bi][:, 320:352].rearrange("p (s j) -> p s j", s=NSQ), func=AF.Copy), (("ps", bi),), (("ue", c),))
                    dve(lambda e: e.tensor_copy(out=TAILA[:, c, :], in_=UE[c][:, 1024:1039]), (("ue", c),), ("taila",))
                    dve(lambda e: e.tensor_copy(out=UEs[c][:, :, 0:15], in_=POOLS[:, c, :, :]), ("pools",), (("ue", c),))
            proj_blocks(l, C_POOL, 4, cons_u)
            exchange(fv(XT0, 120), fv(XT0 + 120, 120), 120, ("taila",), "pasta")
            sp_dma(o_pool_p[:, l * 120:(l + 1) * 120], fv(XT0, 120), reads=("taila",))
            sp_dma(o_pool_so[l], d_pool_nat[l, :, 8:15, :])
            for g in range(4):
                w = 2 << g
                for cc in range(2):
                    c = g * 2 + cc
                    dve(lambda e: e.tensor_copy(out=UE[c][:, 0:15], in_=PASTA[:, c, :]), ("pasta",), (("ue", c),))
                    sp_dma(o_pool_sn[:, (l * 8 + c) * TS:(l * 8 + c + 1) * TS].rearrange("p (s j) -> p s j", s=NSQ), UEs[c][:, :, 15:23], reads=(("ue", c),))
                    cur = UE[c]
                    sh = 1
                    bufs = [S1, S2]
                    bi_ = 0
                    while sh < w:
                        dst = bufs[bi_ % 2]
                        dve(lambda e, cur=cur, dst=dst, sh=sh: e.tensor_tensor(out=dst[:, sh:1131], in0=cur[:, sh:1131], in1=cur[:, 0:1131 - sh], op=ALU.add),
                            (("ue", c), ("ptmp", 0), ("ptmp", 1)), (("ptmp", bi_ % 2),))
                        cur = dst
                        sh *= 2
                        bi_ += 1
                    dve(lambda e, cur=cur: e.scalar_tensor_tensor(out=Dbf[:, cc, 16:TP], in0=cur[:, 31:15 + TP], scalar=1.0 / w, in1=UE[c][:, 31:15 + TP], op0=ALU.mult, op1=ALU.subtract),
                        (("ue", c), ("ptmp", 0), ("ptmp", 1)), (("dbf", cc),))
                    dve(lambda e, cur=cur: e.tensor_tensor(out=S2[:, 0:16] if cur is not S2 else S1[:, 0:16], in0=cur[:, 15:31], in1=CST[:, CS_INV + g * 16:CS_INV + (g + 1) * 16], op=ALU.mult),
                        (("ue", c), ("ptmp", 0), ("ptmp", 1), "cst"), (("ptmp", 0), ("ptmp", 1)))
                    tmp16 = (S2 if cur is not S2 else S1)[:, 0:16]
                    dve(lambda e, tmp16=tmp16: e.tensor_tensor(out=Dbf[:, cc, 0:16], in0=tmp16, in1=UE[c][:, 15:31], op=ALU.subtract),
                        (("ue", c), ("ptmp", 0), ("ptmp", 1)), (("dbf", cc),))
                    curs = cur[:, 1039:1039 + 92].rearrange("p (s j) -> p s j", s=NSQ)
                    dve(lambda e, curs=curs: e.scalar_tensor_tensor(out=Dbf[:, cc, TP:T].rearrange("p (s j) -> p s j", s=NSQ), in0=curs[:, :, 15:23], scalar=1.0 / w, in1=UEs[c][:, :, 15:23], op0=ALU.mult, op1=ALU.subtract),
                        (("ue", c), ("ptmp", 0), ("ptmp", 1)), (("dbf", cc),))
                for dd in range(2):
                    for tg in range(NTG):
                        bi = nextbank()
                        mm(bi, [(PWT[:, g, cc, dd * 128:(dd + 1) * 128], Dbf[:, cc, tgs(tg)]) for cc in range(2)], reads=("pwt", ("dbf", 0), ("dbf", 1)))
                        oc = g * 2 + dd
                        act(lambda e, oc=oc, tg=tg, bi=bi: e.activation(out=BRv[0][:, oc, tgs(tg)], in_=banks[bi][:, 0:TG], func=AF.Copy, scale=vec(l, V_PSC + oc)),
                            (("ps", bi), "vec"), (("br", 0),))

            if MSTOP < 2:
                return
            barrier_all()
            XS2 = XT0 + 1744
            XBW = 1072
            XB = [fv(WS + c * XBW, XBW) for c in range(8)]
            XBs = [XB[c][:, 1027:1027 + 44].rearrange("p (s j) -> p s j", s=NSQ) for c in range(8)]
            LT = WS + 8 * XBW
            TAILB = fv(XS2, 24).rearrange("p (c j) -> p c j", c=8)
            PASTB = fv(XS2 + 24, 24).rearrange("p (c j) -> p c j", c=8)
            CONVS = fv(XS2 + 48, 96).rearrange("p (c s j) -> p c s j", c=8, s=NSQ)
            HS = fv(XS2 + 144, 32).rearrange("p (c s) -> p c s", c=8)
            HLOC = fv(XS2 + 176, 8)
            ALAST = fv(XS2 + 184, 8)
            HIN = fv(XS2 + 192, 8)
            HFIN = fv(XS2 + 200, 8)
            LW = bv(XS2 + 208, 1024).rearrange("p (a c m) -> p a c m", a=2, c=8)
            S.op("pool", lambda e: e.dma_start(out=LW, in_=d_lruw[:, l * 2048:(l + 1) * 2048].rearrange("p (a c m) -> p a c m", a=2, c=8)),
                 writes=("lw",), dsem=misc["lw"])
            sp_dma(CONVS, d_conv_s[:, l * 96:(l + 1) * 96].rearrange("p (c s j) -> p c s j", c=8, s=NSQ), writes=("convs",))
            sp_dma(HS, d_h_s[:, l * 32:(l + 1) * 32].rearrange("p (c s) -> p c s", c=8), writes=("hs",))
            SCA = fv(XS2 + 1232, 8)
            SC2 = fv(XS2 + 1240, 8)
            act(lambda e: e.activation(out=SCA, in_=vec(l, V_LAM, 8), func=AF.Exp, scale=-1.0), ("vec",), ("sca",))
            act(lambda e: e.activation(out=SCA, in_=SCA, func=AF.Ln, bias=1.0), ("sca",), ("sca",))
            dve(lambda e: e.tensor_scalar(out=SC2, in0=SCA, scalar1=-16.0, scalar2=None, op0=ALU.mult), ("sca",), ("sc2",))
            dve(lambda e: e.tensor_scalar(out=SCA, in0=SCA, scalar1=-8.0, scalar2=None, op0=ALU.mult), ("sca", "sc2"), ("sca",))

            def cons_xb(c, tg, bi):
                if tg < 2:
                    act(lambda e: e.activation(out=XB[c][:, 3 + tg * TG:3 + (tg + 1) * TG], in_=banks[bi][:, 0:TG], func=AF.Copy), (("ps", bi),), (("xb", c),))
                else:
                    act(lambda e: e.activation(out=XB[c][:, 3 + 704:3 + 1024], in_=banks[bi][:, 0:320], func=AF.Copy), (("ps", bi),), (("xb", c),))
                    act(lambda e: e.activation(out=XBs[c][:, :, 3:11], in_=banks[bi][:, 320:352].rearrange("p (s j) -> p s j", s=NSQ), func=AF.Copy), (("ps", bi),), (("xb", c),))
                    dve(lambda e: e.tensor_copy(out=TAILB[:, c, :], in_=XB[c][:, 1024:1027]), (("xb", c),), ("tailb",))
                    dve(lambda e: e.tensor_copy(out=XBs[c][:, :, 0:3], in_=CONVS[:, c, :, :]), ("convs",), (("xb", c),))
            proj_blocks(l, C_XL, 4, cons_xb)
            exchange(fv(XS2, 24), fv(XS2 + 24, 24), 24, ("tailb",), "pastb")
            sp_dma(o_conv_p[:, l * 24:(l + 1) * 24], fv(XS2, 24), reads=("tailb",))
            XC = fv(LT, T)
            XCB = bv(LT + 1056, 528)
            RR = fv(LT + 1584, T)
            II = fv(LT + 2640, T)
            AA = fv(LT + 3696, T)
            BB = fv(LT + 4752, T)
            GG = fv(LT + 5808, T)
            HH = fv(LT + 6864, T)
            GT = fv(LT + 7920, TG)
            Y2 = BRv[1]

            for c in range(8):
                sp_dma(o_xb_s[:, (l * 8 + c) * TS:(l * 8 + c + 1) * TS].rearrange("p (s j) -> p s j", s=NSQ), XBs[c][:, :, 3:11], reads=(("xb", c),))
                dve(lambda e: e.tensor_copy(out=XB[c][:, 0:3], in_=PASTB[:, c, :]), ("pastb",), (("xb", c),))
                xbk = ("xb", c)
                cw = lambda j: vec(l, V_CW + c * 4 + j)
                XCs = XC[:, TP:T].rearrange("p (s j) -> p s j", s=NSQ)
                dve(lambda e: e.tensor_scalar(out=XC[:, 0:TP], in0=XB[c][:, 3:3 + TP], scalar1=cw(3), scalar2=vec(l, V_CB + c), op0=ALU.mult, op1=ALU.add), (xbk, "vec"), ("xc",))
                dve(lambda e: e.tensor_scalar(out=XCs, in0=XBs[c][:, :, 3:11], scalar1=cw(3), scalar2=vec(l, V_CB + c), op0=ALU.mult, op1=ALU.add), (xbk, "vec"), ("xc",))
                for j in range(3):
                    dve(lambda e, j=j: e.scalar_tensor_tensor(out=XC[:, 0:TP], in0=XB[c][:, j:j + TP], scalar=cw(j), in1=XC[:, 0:TP], op0=ALU.mult, op1=ALU.add), (xbk, "vec", "xc"), ("xc",))
                    dve(lambda e, j=j: e.scalar_tensor_tensor(out=XCs, in0=XBs[c][:, :, j:j + 8], scalar=cw(j), in1=XCs, op0=ALU.mult, op1=ALU.add), (xbk, "vec", "xc"), ("xc",))
                act(lambda e: e.activation(out=XCB, in_=XC, func=AF.Copy), ("xc",), ("xcb",))
                for tg in range(NTG):
                    ba_, bx_ = nextbank(), nextbank()
                    mm(ba_, [(LW[:, 0, c, :], XCB[:, tgs(tg)])], reads=("lw", "xcb"))
                    mm(bx_, [(LW[:, 1, c, :], XCB[:, tgs(tg)])], reads=("lw", "xcb"))
                    act(lambda e, tg=tg, ba_=ba_: e.activation(out=RR[:, tgs(tg)], in_=banks[ba_][:, 0:TG], func=AF.Sigmoid, bias=vec(l, V_BA + c)), (("ps", ba_), "vec"), (("rr", tg),))
                    act(lambda e, tg=tg, bx_=bx_: e.activation(out=II[:, tgs(tg)], in_=banks[bx_][:, 0:TG], func=AF.Sigmoid, bias=vec(l, V_BX + c)), (("ps", bx_), "vec"), (("ii", tg),))
                rrk = [("rr", tg) for tg in range(NTG)]
                iik = [("ii", tg) for tg in range(NTG)]
                act(lambda e: e.activation(out=AA, in_=RR, func=AF.Exp, scale=SCA[:, c:c + 1]), rrk + ["sca"], ("aa",))
                act(lambda e: e.activation(out=BB, in_=RR, func=AF.Exp, scale=SC2[:, c:c + 1]), rrk + ["sc2"], ("bb",))
                dve(lambda e: e.tensor_scalar(out=BB, in0=BB, scalar1=-1.0, scalar2=1.0, op0=ALU.mult, op1=ALU.add), ("bb",), ("bb",))
                act(lambda e: e.activation(out=BB, in_=BB, func=AF.Sqrt), ("bb",), ("bb",))
                dve(lambda e: e.tensor_tensor(out=II, in0=II, in1=XC, op=ALU.mult), iik + ["xc"], iik)
                dve(lambda e: e.tensor_tensor(out=BB, in0=BB, in1=II, op=ALU.mult), iik + ["bb"], ("bb",))
                dve(lambda e: e.tensor_tensor_scan(out=HH[:, 0:TP], data0=AA[:, 0:TP], data1=BB[:, 0:TP], initial=0.0, op0=ALU.mult, op1=ALU.add), ("aa", "bb"), ("hh",))
                for s in range(NSQ):
                    dve(lambda e, s=s: e.tensor_tensor_scan(out=HH[:, TP + s * 8:TP + (s + 1) * 8], data0=AA[:, TP + s * 8:TP + (s + 1) * 8], data1=BB[:, TP + s * 8:TP + (s + 1) * 8], initial=HS[:, c, s:s + 1], op0=ALU.mult, op1=ALU.add),
                        ("aa", "bb", "hs", "hh"), ("hh",))
                dve(lambda e: e.memset(II[:, 0:TP], 0.0), iik + ["bb"], iik)
                dve(lambda e: e.tensor_tensor_scan(out=RR[:, 0:TP], data0=AA[:, 0:TP], data1=II[:, 0:TP], initial=1.0, op0=ALU.mult, op1=ALU.add), ["aa"] + iik + rrk, rrk)
                dve(lambda e: e.tensor_copy(out=HLOC[:, c:c + 1], in_=HH[:, TP - 1:TP]), ("hh",), ("hloc",))
                dve(lambda e: e.tensor_copy(out=ALAST[:, c:c + 1], in_=RR[:, TP - 1:TP]), rrk, ("alast",))
                sp_dma(o_h_s[:, (l * 8 + c) * TS:(l * 8 + c + 1) * TS], HH[:, TP:T], reads=("hh",))
                src = d_win[l, :, C_GL + c * 128:C_GL + (c + 1) * 128].rearrange("(k p) c -> p k c", p=128)
                sg = st_win.load([(lambda off: bv(off, 2048).rearrange("p (k c) -> p k c", k=KC)[:, :, 0:128], src)])
                for tg in range(NTG):
                    bi = nextbank()
                    mm(bi, [(winv(sg)[:, k, 0:128], Hv[:, k, tgs(tg)]) for k in range(KC)], reads=(("win", sg), ("H",)))
                    act(lambda e, tg=tg, bi=bi: e.activation(out=GG[:, tgs(tg)], in_=banks[bi][:, 0:TG], func=AF.Copy), (("ps", bi),), (("gg", tg),))
                    act(lambda e, tg=tg, bi=bi: e.activation(out=GT, in_=banks[bi][:, 0:TG], func=AF.Square), (("ps", bi),), ("gt",))
                    dve(lambda e: e.tensor_scalar(out=GT, in0=GT, scalar1=0.044715, scalar2=1.0, op0=ALU.mult, op1=ALU.add), ("gt",), ("gt",))
                    dve(lambda e, tg=tg: e.tensor_tensor(out=GT, in0=GT, in1=GG[:, tgs(tg)], op=ALU.mult), ("gt", ("gg", tg)), ("gt",))
                    act(lambda e: e.activation(out=GT, in_=GT, func=AF.Sigmoid, scale=1.5957691216057308), ("gt",), ("gt",))
                    dve(lambda e, tg=tg: e.tensor_tensor(out=GG[:, tgs(tg)], in0=GG[:, tgs(tg)], in1=GT, op=ALU.mult), ("gt", ("gg", tg)), (("gg", tg),))
                ggk = [("gg", tg) for tg in range(NTG)]
                dve(lambda e: e.tensor_tensor(out=BRv[2][:, c, :], in0=HH, in1=GG, op=ALU.mult), ["hh"] + ggk, (("br", 2, c),))
                dve(lambda e: e.tensor_tensor(out=Y2[:, c, 0:TP], in0=RR[:, 0:TP], in1=GG[:, 0:TP], op=ALU.mult), rrk + ggk, (("y2", c),))
            exchange(HLOC, HIN, 8, ("hloc",), "hin")
            for c in range(8):
                dve(lambda e, c=c: e.scalar_tensor_tensor(out=BRv[2][:, c, 0:TP], in0=Y2[:, c, 0:TP], scalar=HIN[:, c:c + 1], in1=BRv[2][:, c, 0:TP], op0=ALU.mult, op1=ALU.add),
                    (("y2", c), "hin", ("br", 2, c)), (("br", 2, c),))
            dve(lambda e: e.tensor_tensor(out=HFIN, in0=ALAST, in1=HIN, op=ALU.mult), ("alast", "hin"), ("hfin",))
            dve(lambda e: e.tensor_tensor(out=HFIN, in0=HFIN, in1=HLOC, op=ALU.add), ("hfin", "hloc"), ("hfin",))
            sp_dma(o_h_p[:, l * 8:(l + 1) * 8], HFIN, reads=("hfin",))

            if MSTOP < 3:
                return
            barrier_all()
            KDW = 592
            KD = [bv(WS + j * KDW, KDW) for j in range(4)]
            VD = bv(WS + 2368, 2560).rearrange("p (t j r d) -> p t j r d", t=10, j=4, r=2)
            QQ = bv(WS + 4928, 4224).rearrange("p (c t) -> p c t", c=8)
            COS = fv(A_SQ, T)
            SIN = fv(A_RSTD, T)
            AT = WS + 9152
            QG = fv(AT, TG)
            T1 = fv(AT + 352, TG)
            T2 = fv(AT + 704, TG)
            RS = fv(AT + 1056, TG)
            SQB = bv(AT + 1408, 176)
            KF = fv(AT + 1584, TG)
            PT_ = [bv(AT + 1936 + i * 256, 256) for i in range(4)]
            RD = [fv(AT + 2960 + i * 128, 128) for i in range(2)]
            TAILK = fv(AT + 3216, 512).rearrange("p (j t) -> p j t", j=4)
            TAILV = fv(AT + 3728, 256)
            PASTKV = fv(AT + 3984, 768)
            KSN = fv(AT + 4752, 128).rearrange("p (j t) -> p j t", j=4)
            VSN = fv(AT + 4880, 256)
            KCT = bv(AT + 5136, 1024).rearrange("p (s j t) -> p s j t", s=NSQ, j=4)
            VCD = bv(AT + 6160, 1024).rearrange("p (s j d) -> p s j d", s=NSQ, j=4)
            sp_dma(COS, d_cos[:, :], writes=("cos",))
            sp_dma(SIN, d_sin[:, :], writes=("sin",))
            S.op("pool", lambda e: e.dma_start(out=KCT, in_=d_kcT[:, l * 2048:(l + 1) * 2048].rearrange("p (s j t) -> p s j t", s=NSQ, j=4)), writes=("kct",), dsem=misc["kct"])
            S.op("pool", lambda e: e.dma_start(out=VCD, in_=d_vc[:, l * 2048:(l + 1) * 2048].rearrange("p (s j d) -> p s j d", s=NSQ, j=4)), writes=("vcd",), dsem=misc["vcd"])
            sp_dma(o_k_so[l], d_kc_nat[l, :, 8:128, :])
            sp_dma(o_v_so[l], d_vc_nat[l, :, 8:128, :])
            act(lambda e: e.activation(out=ESNK, in_=vec(l, V_SNK, 16), func=AF.Exp), ("vec",), ("esnk",))

            ROTB = bv(AT + 7184, 64)
            QGB = bv(AT + 7248, 176)
            dve(lambda e: e.tensor_copy(out=ROTB, in_=ROTM), ("cst",), ("rotb",))
            S3SUB = int(_os.environ.get('S3SUB', '9'))
            if S3SUB < 1:
                return

            def qk_finish(bi, tg, gvec, dst_bf, dst_f32=None):
                if S3SUB < 2:
                    act(lambda e: e.activation(out=dst_bf, in_=banks[bi][:, 0:TG], func=AF.Copy), (("ps", bi),), ("kq",))
                    if dst_f32 is not None:
                        act(lambda e: e.activation(out=dst_f32, in_=banks[bi][:, 0:TG], func=AF.Copy), (("ps", bi), "kf"), ("kf",))
                    return
                act(lambda e: e.activation(out=QG, in_=banks[bi][:, 0:TG], func=AF.Copy, scale=gvec), (("ps", bi), "vec", "qg"), ("qg",))
                act(lambda e: e.activation(out=SQB, in_=banks[bi][:, 0:TG], func=AF.Square), (("ps", bi), "sqb"), ("sqb",))
                b2, b3 = nextbank(), nextbank()
                mm(b2, [(OBD, SQB)], reads=("cb", "sqb"))
                if _os.environ.get('ROTF32'):
                    mm(b3, [(ROTM, QG)], reads=("cst", "qg"))
                else:
                    act(lambda e: e.activation(out=QGB, in_=banks[bi][:, 0:TG], func=AF.Copy, scale=gvec), (("ps", bi), "vec", "qgb"), ("qgb",))
                    mm(b3, [(ROTB, QGB)], reads=("rotb", "qgb"))
                act(lambda e: e.activation(out=RS, in_=banks[b2][:, 0:TG], func=AF.Sqrt, scale=1.0 / 64, bias=EPSB), (("ps", b2), "epsb", "rs"), ("rs",))
                dve(lambda e: e.reciprocal(out=RS, in_=RS), ("rs",), ("rs",))
                dve(lambda e: e.tensor_tensor(out=T1, in0=QG, in1=COS[:, tgs(tg)], op=ALU.mult), ("qg", "cos", "t1"), ("t1",))
                dve(lambda e: e.tensor_tensor(out=T2, in0=banks[b3][:, 0:TG], in1=SIN[:, tgs(tg)], op=ALU.mult), (("ps", b3), "sin", "t2"), ("t2",))
                dve(lambda e: e.tensor_tensor(out=T1, in0=T1, in1=T2, op=ALU.add), ("t1", "t2"), ("t1",))
                if dst_f32 is not None:
                    dve(lambda e: e.tensor_tensor(out=dst_f32, in0=T1, in1=RS, op=ALU.mult), ("t1", "rs", "kf"), ("kf",))
                    act(lambda e: e.activation(out=dst_bf, in_=dst_f32, func=AF.Copy), ("kf",), ("kq",))
                else:
                    dve(lambda e: e.tensor_tensor(out=dst_bf, in0=T1, in1=RS, op=ALU.mult), ("t1", "rs"), ("kq",))

            for hp in range(2):
                def kdst(off, hp=hp):
                    return bv(off, 2048).rearrange("p (k j r d) -> p k j r d", k=KC, j=2, r=2)
                parts = []
                for jj in range(2):
                    j = hp * 2 + jj
                    src = d_win[l, :, C_K + j * 64:C_K + (j + 1) * 64].rearrange("(k p) c -> p k c", p=128)
                    for r in range(2):
                        parts.append((lambda off, jj=jj, r=r, hp=hp: bv(off, 2048).rearrange("p (k j r d) -> p k j r d", k=KC, j=2, r=2)[:, :, jj, r, :], src))
                sk = st_win.load(parts)
                wk = bv(WIN_OFF[sk], 2048).rearrange("p (k j m) -> p k j m", k=KC, j=2)
                for jj in range(2):
                    j = hp * 2 + jj
                    for tg in range(NTG):
                        bi = nextbank()
                        mm(bi, [(wk[:, k, jj, :], Hv[:, k, tgs(tg)]) for k in range(KC)], reads=(("win", sk), ("H",)))
                        qk_finish(bi, tg, vec(l, V_KN), KD[j][:, 128 + tg * TG:128 + (tg + 1) * TG], dst_f32=KF)
                        if tg == 2:
                            dve(lambda e, j=j: e.tensor_copy(out=TAILK[:, j, :], in_=KF[:, 192:320]), ("kf",), ("tailk",))
                            dve(lambda e, j=j: e.tensor_copy(out=KSN[:, j, :], in_=KF[:, 320:352]), ("kf",), ("ksn",))
            if S3SUB < 3:
                return
            sv = win_block(l, C_V)
            wv_ = winv(sv)
            for tt in range(int(_os.environ.get('VTT', '9'))):
                ntk = 128 if tt < 8 else TS
                bi = nextbank()
                ov = banks[bi][0:ntk, 0:256]
                mm(bi, [(Hv[:, k, tt * 128:tt * 128 + ntk], wv_[:, k, :]) for k in range(KC)], reads=(("win", sv), ("H",)), out_ap=ov)
                ov4 = ov.rearrange("p (j d) -> p j d", j=4)
                act(lambda e, tt=tt, ov4=ov4, ntk=ntk: e.activation(out=VD[0:ntk, tt + 1, :, 0, :], in_=ov4, func=AF.Copy), (("ps", bi),), ("vd",))
                act(lambda e, tt=tt, ov4=ov4, ntk=ntk: e.activation(out=VD[0:ntk, tt + 1, :, 1, :], in_=ov4, func=AF.Copy), (("ps", bi),), ("vd",))
                if tt == 7:
                    act(lambda e, ov=ov: e.activation(out=TAILV, in_=ov, func=AF.Copy), (("ps", bi),), ("tailv",))
                if tt == 8:
                    act(lambda e, ov=ov: e.activation(out=VSN[0:TS, :], in_=ov, func=AF.Copy), (("ps", bi),), ("vsn",))
            if S3SUB < 4:
                return
            exchange(fv(AT + 3216, 768), PASTKV, 768, ("tailk", "tailv"), "pastkv")
            sp_dma(o_k_p[:, l * 512:(l + 1) * 512], fv(AT + 3216, 512), reads=("tailk",))
            sp_dma(o_v_p[:, l * 256:(l + 1) * 256], TAILV, reads=("tailv",))
            sp_dma(o_k_sn[:, l * 128:(l + 1) * 128], fv(AT + 4752, 128), reads=("ksn",))
            sp_dma(o_v_sn[:, l * 256:(l + 1) * 256], VSN[0:TS, :], reads=("vsn",))
            for j in range(4):
                dve(lambda e, j=j: e.tensor_copy(out=KD[j][:, 0:128], in_=PASTKV[:, j * 128:(j + 1) * 128]), ("pastkv",), ("kq",))
            pv4 = PASTKV[:, 512:768].rearrange("p (j d) -> p j d", j=4)
            for r in range(2):
                dve(lambda e, r=r: e.tensor_copy(out=VD[:, 0, :, r, :], in_=pv4), ("pastkv",), ("vd",))
            if int(_os.environ.get('S3STOP', '9')) < 2:
                return
            if not _os.environ.get('NOQBAR'):
                barrier_all()
            def cons_q(c, tg, bi):
                qk_finish(bi, tg, vec(l, V_QN), QQ[:, c, tgs(tg)])
            proj_blocks(l, C_Q, 4, cons_q)

            if int(_os.environ.get('S3STOP', '9')) < 3:
                return
            pti = [0]
            for n in range(8):
                for j in range(4):
                    pts = []
                    for kt in range(2):
                        b2_ = [nextbank(), nextbank()]
                        kcols = slice(n * 128 + kt * 128, n * 128 + kt * 128 + 128)
                        for hf in range(2):
                            prt = slice(hf * 64, hf * 64 + 64)
                            o = banks[b2_[hf]][:, 0:256].rearrange("p (c q) -> p c q", c=2)
                            S.op("pe", lambda e, o=o, prt=prt, kcols=kcols: e.matmul(o, KD[j][prt, kcols], QQ[prt, 2 * j:2 * j + 2, n * 128:(n + 1) * 128], start=True, stop=True),
                                 reads=("kq",), writes=(("ps", b2_[hf]),), inc=True)
                        pt = PT_[pti[0] % 4]
                        pk = ("pt", pti[0] % 4)
                        pti[0] += 1
                        for hf in range(2):
                            act(lambda e, pt=pt, hf=hf, bb=b2_[hf]: e.activation(out=pt[:, hf * 256:(hf + 1) * 256], in_=banks[bb][:, 0:256], func=AF.Exp, scale=0.125), (("ps", b2_[hf]), pk), (pk,))
                        msk = (MP0 if n == 0 else MP) if kt == 0 else MC
                        dve(lambda e, pt=pt, msk=msk: e.tensor_tensor(out=pt, in0=pt, in1=msk, op=ALU.mult), (pk, "cb"), (pk,))
                        pts.append((pt, pk))
                    bo, bd = nextbank(), nextbank()
                    mm(bo, [(VD[:, n + kt, j, :, :], pts[kt][0]) for kt in range(2)], reads=("vd", pts[0][1], pts[1][1]), out_ap=banks[bo][:, 0:512])
                    mm(bd, [(ONES, pts[kt][0]) for kt in range(2)], reads=("cb", pts[0][1], pts[1][1]), out_ap=banks[bd][:, 0:512])
                    for hh in range(4):
                        hf, cq = hh // 2, hh % 2
                        head = 4 * j + 2 * cq + hf
                        prt = slice(hf * 64, hf * 64 + 64)
                        rd = RD[hh % 2]
                        rk = ("rd", hh % 2)
                        dve(lambda e, rd=rd, prt=prt, hh=hh, head=head: e.tensor_scalar(out=rd[prt, :], in0=banks[bd][prt, hh * 128:(hh + 1) * 128], scalar1=ESNK[prt, head:head + 1], scalar2=None, op0=ALU.add),
                            (("ps", bd), "esnk", rk), (rk,))
                        dve(lambda e, rd=rd, prt=prt: e.reciprocal(out=rd[prt, :], in_=rd[prt, :]), (rk,), (rk,))
                        dve(lambda e, rd=rd, prt=prt, hh=hh, cq=cq: e.tensor_tensor(out=BRv[1][prt, 2 * j + cq, n * 128:(n + 1) * 128], in0=banks[bo][prt, hh * 128:(hh + 1) * 128], in1=rd[prt, :], op=ALU.mult),
                            (("ps", bo), rk), (("br", 1),))
            if int(_os.environ.get('S3STOP', '9')) < 4:
                return
            for j in range(4):
                bc = [nextbank(), nextbank()]
                bn = [nextbank(), nextbank()]
                for s in range(NSQ):
                    for hf in range(2):
                        prt = slice(hf * 64, hf * 64 + 64)
                        o = banks[bc[hf]][:, s * 16:(s + 1) * 16].rearrange("p (c q) -> p c q", c=2)
                        S.op("pe", lambda e, o=o, prt=prt, s=s: e.matmul(o, KCT[prt, s, j, :], QQ[prt, 2 * j:2 * j + 2, TP + s * 8:TP + (s + 1) * 8], start=True, stop=True),
                             reads=("kct", "kq"), writes=(("ps", bc[hf]),), inc=True)
                for hf in range(2):
                    prt = slice(hf * 64, hf * 64 + 64)
                    o = banks[bn[hf]][0:TS, 0:64].rearrange("p (c q) -> p c q", c=2)
                    S.op("pe", lambda e, o=o, prt=prt: e.matmul(o, KD[j][prt, 1152:1184], QQ[prt, 2 * j:2 * j + 2, TP:T], start=True, stop=True),
                         reads=("kq",), writes=(("ps", bn[hf]),), inc=True)
                pc = PT_[0][:, 0:128]
                pn = PT_[1][0:TS, 0:128]
                pc4 = pc.rearrange("p (s x) -> p s x", s=NSQ)
                for hf in range(2):
                    act(lambda e, hf=hf: e.activation(out=pc4[:, :, hf * 16:(hf + 1) * 16], in_=banks[bc[hf]][:, 0:64].rearrange("p (s x) -> p s x", s=NSQ), func=AF.Exp, scale=0.125), (("ps", bc[hf]), ("pt", 0)), (("pt", 0),))
                    act(lambda e, hf=hf: e.activation(out=pn[:, hf * 64:(hf + 1) * 64], in_=banks[bn[hf]][0:TS, 0:64], func=AF.Exp, scale=0.125), (("ps", bn[hf]), ("pt", 1)), (("pt", 1),))
                pc3 = pc.rearrange("p (s x) -> p s x", s=NSQ)
                for s in range(NSQ):
                    dve(lambda e, s=s: e.tensor_tensor(out=pc3[:, s, :], in0=pc3[:, s, :], in1=MSC, op=ALU.mult), (("pt", 0), "cb"), (("pt", 0),))
                dve(lambda e: e.tensor_tensor(out=pn, in0=pn, in1=MSN[0:TS, :], op=ALU.mult), (("pt", 1), "cb"), (("pt", 1),))
                bo, bd = nextbank(), nextbank()
                S.op("pe", lambda e: e.matmul(banks[bo][:, 0:128], VD[0:TS, 9, j, :, :], pn, start=True, stop=False), reads=("vd", ("pt", 1)), writes=(("ps", bo),), inc=False)
                S.op("pe", lambda e: e.matmul(banks[bd][:, 0:128], ONES[0:TS, :], pn, start=True, stop=False), reads=("cb", ("pt", 1)), writes=(("ps", bd),), inc=False)
                for s in range(NSQ):
                    oo = banks[bo][:, 0:128].rearrange("p (h q) -> p h q", h=4)[:, :, s * 8:(s + 1) * 8]
                    od = banks[bd][:, 0:128].rearrange("p (h q) -> p h q", h=4)[:, :, s * 8:(s + 1) * 8]
                    rhs = pc3[:, s, :].rearrange("p (h q) -> p h q", h=4)
                    last = s == NSQ - 1
                    S.op("pe", lambda e, oo=oo, rhs=rhs, s=s, last=last: e.matmul(oo, VCD[:, s, j, :], rhs, start=False, stop=last, skip_group_check=True), reads=("vcd", ("pt", 0)), inc=False)
                    S.op("pe", lambda e, od=od, rhs=rhs, last=last: e.matmul(od, ONES, rhs, start=False, stop=last, skip_group_check=True), reads=("cb", ("pt", 0)), inc=last)
                tk = Tok(S.E["pe"]["sem"], S.E["pe"]["cnt"], "pe", S.E["pe"]["sid"])
                S.lastw[("ps", bo)] = tk
                S.lastw[("ps", bd)] = tk
                S.readers.setdefault(("pt", 0), {})[tk.sid] = tk
                S.readers.setdefault(("pt", 1), {})[tk.sid] = tk
                for hh in range(4):
                    hf, cq = hh // 2, hh % 2
                    head = 4 * j + 2 * cq + hf
                    prt = slice(hf * 64, hf * 64 + 64)
                    rd = RD[hh % 2]
                    rk = ("rd", hh % 2)
                    dve(lambda e, rd=rd, prt=prt, hh=hh, head=head: e.tensor_scalar(out=rd[prt, 0:TS], in0=banks[bd][prt, hh * 32:(hh + 1) * 32], scalar1=ESNK[prt, head:head + 1], scalar2=None, op0=ALU.add),
                        (("ps", bd), "esnk", rk), (rk,))
                    dve(lambda e, rd=rd, prt=prt: e.reciprocal(out=rd[prt, 0:TS], in_=rd[prt, 0:TS]), (rk,), (rk,))
                    dve(lambda e, rd=rd, prt=prt, hh=hh, cq=cq: e.tensor_tensor(out=BRv[1][prt, 2 * j + cq, TP:T], in0=banks[bo][prt, hh * 32:(hh + 1) * 32], in1=rd[prt, 0:TS], op=ALU.mult),
                        (("ps", bo), rk), (("br", 1),))

            if MSTOP < 4:
                return
            barrier_all()
            brkeys = [("br", 0), ("br", 1)] + [("br", 2, c) for c in range(8)]

            def load_gate(dch):
                parts = []
                for b in range(3):
                    src = d_win[l, :, C_GATE + b * D + dch * 128:C_GATE + b * D + (dch + 1) * 128].rearrange("(k p) c -> p k c", p=128)
                    parts.append((lambda off, b=b: bv(off, 3072).rearrange("p (b k c) -> p b k c", b=3, k=KC)[:, b, :, :], src))
                return st_gate.load(parts)

            def load_brw(dch):
                parts = []
                for b in range(3):
                    src = d_wbr[b][l, :, dch * 128:(dch + 1) * 128].rearrange("(k p) c -> p k c", p=128)
                    parts.append((lambda off, b=b: bv(off, 2048)[:, 0:3072].rearrange("p (b k c) -> p b k c", b=3, k=8)[:, b, :, :], src))
                return st_win.load(parts)

            gs = {0: load_gate(0)}
            ws = {0: load_brw(0)}
            SG = fv(XT0 + 3400, TG)
            MT = fv(XT0 + 3752, TG)
            for dch in range(KC):
                if dch + 1 < KC:
                    gs[dch + 1] = load_gate(dch + 1)
                    ws[dch + 1] = load_brw(dch + 1)
                gv = bv(GATE_OFF[gs[dch]], 3072).rearrange("p (b k c) -> p b k c", b=3, k=KC)
                bw = bv(WIN_OFF[ws[dch]], 2048)[:, 0:3072].rearrange("p (b k c) -> p b k c", b=3, k=8)
                for tg in range(NTG):
                    for b in range(3):
                        bg, bb = nextbank(), nextbank()
                        mm(bg, [(gv[:, b, k, :], Hv[:, k, tgs(tg)]) for k in range(KC)], reads=(("gate", gs[dch]), ("H",)))
                        mm(bb, [(bw[:, b, k, :], BRv[b][:, k, tgs(tg)]) for k in range(8)], reads=(("win", ws[dch]),) + tuple(brkeys))
                        act(lambda e, bg=bg: e.activation(out=SG, in_=banks[bg][:, 0:TG], func=AF.Sigmoid), (("ps", bg), "sg"), ("sg",))
                        if b == 0:
                            dve(lambda e, bb=bb: e.tensor_tensor(out=MT, in0=SG, in1=banks[bb][:, 0:TG], op=ALU.mult), ("sg", ("ps", bb), "mt"), ("mt",))
                        else:
                            dve(lambda e, bb=bb: e.tensor_tensor(out=SG, in0=SG, in1=banks[bb][:, 0:TG], op=ALU.mult), ("sg", ("ps", bb)), ("sg",))
                            if b == 1:
                                dve(lambda e: e.tensor_tensor(out=MT, in0=MT, in1=SG, op=ALU.add), ("sg", "mt"), ("mt",))
                            else:
                                dve(lambda e, dch=dch, tg=tg: e.tensor_tensor(out=MERGED[:, dch, tgs(tg)], in0=MT, in1=SG, op=ALU.add), ("sg", "mt"), (("merged",),))

        def outproj(l):
            def wblock(c0):
                src = d_wout[l, :, c0:c0 + 256].rearrange("(k p) c -> p k c", p=128)
                return st_win.load([(lambda off: bv(off, 2048).rearrange("p (k c) -> p k c", k=KC), src)])
            slots = {0: wblock(0)}
            for b in range(8):
                if b + 1 < 8:
                    slots[b + 1] = wblock((b + 1) * 256)
                wv = winv(slots[b])
                for ci in range(2):
                    d = b * 2 + ci
                    for tg in range(NTG):
                        bi = nextbank()
                        mm(bi, [(wv[:, k, ci * 128:(ci + 1) * 128], MERGED[:, k, tgs(tg)]) for k in range(KC)], reads=(("win", slots[b]), ("merged",)))
                        dve(lambda e, bi=bi, d=d, tg=tg: e.tensor_tensor(out=Xv[:, d, tgs(tg)], in0=Xv[:, d, tgs(tg)], in1=banks[bi][:, 0:TG], op=ALU.add),
                            (("ps", bi), Xk(d, tg)), (Xk(d, tg),))

        allX = [Xk(k, tg) for k in range(KC) for tg in range(NTG)]

        import os
        STOP = int(os.environ.get("KSTOP", "999"))
        nph = [0]

        def ph():
            nph[0] += 1
            return nph[0] <= STOP

        for l in range(LN):
            g_ = ffn(l, 0)
            next(g_)
            rmsnorm(l, V_NFFA)
            for _ in g_:
                pass
            if ph(): rmsnorm(l, V_NMIX)
            if ph():
                barrier_all()
                sp_dma(xspill[:, :], fv(X0, KC * T), reads=allX, writes=("xspill",))
                barrier_all()
                mixer(l)
                barrier_all()
                sp_dma(fv(X0, KC * T), xspill[:, :], reads=("xspill",), writes=allX)
            if ph(): outproj(l)
            g_ = ffn(l, 1)
            next(g_)
            rmsnorm(l, V_NFFB)
            for _ in g_:
                pass
        sp_dma(o_y[:, :], fv(X0, KC * T), reads=allX)
        barrier_all(engs=("sp",))
    return nc


NL = DEPTH
_CACHE = {}


def _fm(a, nch):
    sh = a.shape[:-1]
    r = a.reshape(sh + (nch, 128))
    nd = r.ndim
    return np.ascontiguousarray(np.transpose(r, (nd - 1,) + tuple(range(nd - 1))))


def _consts(hf):
    cst = np.zeros((128, NCST), np.float32)
    flag = float(hf)
    cst[:, CS_FLAG] = flag
    for g in range(4):
        w = 2 << g
        for t in range(16):
            cst[:, CS_INV + g * 16 + t] = 1.0 / (min(t + 1, w) if hf == 0 else w)
    for m in range(128):
        d = m % 64
        base = m - d
        if d < 8:
            cst[base + d + 8, CS_ROT + m] = -1.0
        elif d < 16:
            cst[base + d - 8, CS_ROT + m] = 1.0
    k = np.arange(128)
    cst[:, CS_OBD:CS_OBD + 128] = (k[:, None] // 64 == k[None, :] // 64).astype(np.float32)
    mp = (k[None, :] <= k[:, None]).astype(np.float32)
    mc = (k[None, :] >= k[:, None]).astype(np.float32)
    cst[:, CS_MP0:CS_MP0 + 128] = mp * flag
    cst[:, CS_MP:CS_MP + 128] = mp
    cst[:, CS_MC:CS_MC + 128] = mc
    cst[:, CS_MSC:CS_MSC + 8] = (k[:, None] >= np.arange(8)[None, :]).astype(np.float32)
    t32 = np.arange(32)
    msn = ((t32[:, None] // 8 == t32[None, :] // 8) & (t32[:, None] % 8 <= t32[None, :] % 8)).astype(np.float32)
    cst[0:32, CS_MSN:CS_MSN + 32] = msn
    pos = np.concatenate([hf * TP + np.arange(TP), PAST + (np.arange(TS) % SQ)]).astype(np.float32)
    inv = (np.float32(500000.0) ** (-np.arange(8, dtype=np.float32) / np.float32(8))).astype(np.float32)
    ang = (pos[:, None] * inv[None, :]).astype(np.float32)
    cosT = np.ones((128, T), np.float32)
    sinT = np.zeros((128, T), np.float32)
    for p in range(128):
        d = p % 64
        if d < 16:
            cosT[p] = np.cos(ang[:, d % 8])
            sinT[p] = np.sin(ang[:, d % 8])
    return cst, cosT, sinT


def kernel(**inputs):
    LN = NL
    f = lambda a: np.ascontiguousarray(np.asarray(a, dtype=np.float32))
    inp = {k: f(v) for k, v in inputs.items()}
    if LN not in _CACHE:
        _CACHE[LN] = build(LN)
    nc = _CACHE[LN]
    W = {k: inp[k][:LN] for k in inp if k not in ("x_prompt", "x_sample")}
    poolw = np.ascontiguousarray(np.transpose(W["pool_w"].reshape(LN, 4, 2, 128, 256), (3, 0, 1, 2, 4))).reshape(128, -1)
    lruw = np.zeros((128, LN, 2, 8, 128), np.float32)
    for a, nm in enumerate(("lru_gate_a_w", "lru_gate_x_w")):
        for blk in range(16):
            c, hb = blk // 2, blk % 2
            lruw[hb * 64:(hb + 1) * 64, :, a, c, hb * 64:(hb + 1) * 64] = np.transpose(W[nm][:, blk], (1, 0, 2))
    lruw = lruw.reshape(128, -1)
    vecs = np.zeros((128, LN, VL), np.float32)
    for l in range(LN):
        vecs[:, l, V_NFFA:V_NFFA + 16] = W["norm_ffa"][l].reshape(16, 128).T
        vecs[:, l, V_NMIX:V_NMIX + 16] = W["norm_mix"][l].reshape(16, 128).T
        vecs[:, l, V_NFFB:V_NFFB + 16] = W["norm_ffb"][l].reshape(16, 128).T
        vecs[:, l, V_PSC:V_PSC + 8] = W["pool_scale"][l].reshape(8, 128).T
        vecs[:, l, V_CW:V_CW + 32] = np.transpose(W["conv_w"][l].reshape(4, 8, 128), (2, 1, 0)).reshape(128, 32)
        vecs[:, l, V_CB:V_CB + 8] = W["conv_b"][l].reshape(8, 128).T
        vecs[:, l, V_BA:V_BA + 8] = W["lru_gate_a_b"][l].reshape(8, 128).T
        vecs[:, l, V_BX:V_BX + 8] = W["lru_gate_x_b"][l].reshape(8, 128).T
        vecs[:, l, V_LAM:V_LAM + 8] = W["lru_lambda"][l].reshape(8, 128).T
        vecs[:, l, V_QN] = np.tile(W["q_norm"][l], 2)
        vecs[:, l, V_KN] = np.tile(W["k_norm"][l], 2)
        vecs[:, l, V_SNK:V_SNK + 16] = W["attn_sinks"][l][None, :]
    vecs = vecs.reshape(128, -1)
    shared = {
        "ffa_w_gu": W["ffa_w_gu"], "ffb_w_gu": W["ffb_w_gu"], "ffa_w_down": W["ffa_w_down"], "ffb_w_down": W["ffb_w_down"],
        "w_in": W["w_in"], "w_branch_pool": W["w_branch_pool"], "w_branch_attn": W["w_branch_attn"], "w_branch_lru": W["w_branch_lru"],
        "w_out": W["w_out"], "poolw": poolw, "lruw": lruw, "vecs": vecs,
    }
    cs = [_consts(0), _consts(1)]
    in_maps = []
    for c in range(NCORES):
        b, hf = c // 2, c % 2
        xa = np.concatenate([inp["x_prompt"][b, hf * TP:(hf + 1) * TP], inp["x_sample"][NSQ * c:NSQ * (c + 1)].reshape(TS, D)], axis=0)
        m = dict(shared)
        m["xT"] = np.ascontiguousarray(np.transpose(xa.reshape(T, KC, 128), (2, 1, 0))).reshape(128, -1)
        m["cst"], m["cosT"], m["sinT"] = cs[hf]
        sl = slice(NSQ * c, NSQ * (c + 1))
        m["pool_s"] = np.ascontiguousarray(np.transpose(W["state_pool"][:, sl].reshape(LN, NSQ, 15, 8, 128), (4, 0, 3, 1, 2))).reshape(128, -1)
        m["conv_s"] = np.ascontiguousarray(np.transpose(W["state_conv"][:, sl].reshape(LN, NSQ, 3, 8, 128), (4, 0, 3, 1, 2))).reshape(128, -1)
        m["h_s"] = np.ascontiguousarray(np.transpose(W["state_rglru"][:, sl].reshape(LN, NSQ, 8, 128), (3, 0, 2, 1))).reshape(128, -1)
        kT = np.transpose(W["cache_k_win"][:, sl], (4, 0, 1, 3, 2))
        m["kcT"] = np.ascontiguousarray(np.concatenate([kT, kT], axis=0)).reshape(128, -1)
        vv = np.transpose(W["cache_v_win"][:, sl], (2, 0, 1, 3, 4))
        m["vc"] = np.ascontiguousarray(np.concatenate([vv, vv], axis=-1)).reshape(128, -1)
        m["kc_nat"] = np.ascontiguousarray(W["cache_k_win"][:, sl].reshape(LN, NSQ, 128, 256))
        m["vc_nat"] = np.ascontiguousarray(W["cache_v_win"][:, sl].reshape(LN, NSQ, 128, 256))
        m["pool_nat"] = np.ascontiguousarray(W["state_pool"][:, sl])
        in_maps.append(m)
    import os as _os1
    for nm in [x for x in _os1.environ.get("KSMALL", "").split(",") if x]:
        for m in in_maps:
            m[nm] = np.zeros((1, 16), np.float32)
    res = run_bass_kernel_spmd(nc, in_maps, core_ids=list(range(NCORES)))
    R = res.results
    B = 4
    y_p = np.zeros((B, 2 * TP, D), np.float32)
    y_s = np.zeros((NCORES * NSQ, SQ, D), np.float32)
    pool_p = np.zeros((LN, B, 15, 1024), np.float32)
    pool_s = np.zeros((LN, 32, 15, 1024), np.float32)
    k_p = np.zeros((LN, B, 128, 4, 64), np.float32)
    k_s = np.zeros((LN, 32, 128, 4, 64), np.float32)
    v_p = np.zeros((LN, B, 128, 4, 64), np.float32)
    v_s = np.zeros((LN, 32, 128, 4, 64), np.float32)
    conv_p = np.zeros((LN, B, 3, 1024), np.float32)
    conv_s = np.zeros((LN, 32, 3, 1024), np.float32)
    h_p = np.zeros((LN, B, 1024), np.float32)
    h_s = np.zeros((LN, 32, 1024), np.float32)
    for c in range(NCORES):
        b, hf = c // 2, c % 2
        r = R[c]
        sl = slice(NSQ * c, NSQ * (c + 1))
        y = np.transpose(np.asarray(r["yT"]).reshape(128, KC, T), (2, 1, 0)).reshape(T, D)
        y_p[b, hf * TP:(hf + 1) * TP] = y[:TP]
        y_s[sl] = y[TP:].reshape(NSQ, SQ, D)
        pool_s[:, sl, 0:7] = np.asarray(r["o_pool_so"])
        pool_s[:, sl, 7:15] = np.transpose(np.asarray(r["o_pool_sn"]).reshape(128, LN, 8, NSQ, SQ), (1, 3, 4, 2, 0)).reshape(LN, NSQ, SQ, 1024)
        k_s[:, sl, 0:120] = np.asarray(r["o_k_so"]).reshape(LN, NSQ, 120, 4, 64)
        k_s[:, sl, 120:128] = np.transpose(np.asarray(r["o_k_sn"]).reshape(128, LN, 4, NSQ, SQ)[0:64], (1, 3, 4, 2, 0))
        v_s[:, sl, 0:120] = np.asarray(r["o_v_so"]).reshape(LN, NSQ, 120, 4, 64)
        v_s[:, sl, 120:128] = np.transpose(np.asarray(r["o_v_sn"]).reshape(NSQ, SQ, LN, 4, 64), (2, 0, 1, 3, 4))
        conv_s[:, sl] = np.transpose(np.asarray(r["o_xb_s"]).reshape(128, LN, 8, NSQ, SQ)[..., 5:8], (1, 3, 4, 2, 0)).reshape(LN, NSQ, 3, 1024)
        h_s[:, sl] = np.transpose(np.asarray(r["o_h_s"]).reshape(128, LN, 8, NSQ, SQ)[..., 7], (1, 3, 2, 0)).reshape(LN, NSQ, 1024)
        if hf == 1:
            pool_p[:, b] = np.transpose(np.asarray(r["o_pool_p"]).reshape(128, LN, 8, 15), (1, 3, 2, 0)).reshape(LN, 15, 1024)
            k_p[:, b] = np.transpose(np.asarray(r["o_k_p"]).reshape(128, LN, 4, 128)[0:64], (1, 3, 2, 0))
            v_p[:, b] = np.transpose(np.asarray(r["o_v_p"]).reshape(128, LN, 4, 64), (1, 0, 2, 3))
            conv_p[:, b] = np.transpose(np.asarray(r["o_conv_p"]).reshape(128, LN, 8, 3), (1, 3, 2, 0)).reshape(LN, 3, 1024)
            h_p[:, b] = np.transpose(np.asarray(r["o_h_p"]).reshape(128, LN, 8), (1, 2, 0)).reshape(LN, 1024)
    return (y_p, y_s, pool_p, pool_s, k_p, k_s, v_p, v_s, conv_p, conv_s, h_p, h_s)
```
